# Optimizing a Trainium2 kernel written in Bass

```python
import jax, jax.numpy as jnp
from jax import lax
import numpy as np

D_MODEL = 1024
BATCH = 32
SEQ = 256
DEPTH = 2
DEC_BATCH = 4
DEC_SEQ = 1024
PAST_LEN = 512

GRID_W = 64
N_MIXERS = 2
N_ATTN_LAYERS = (DEPTH + 1) // 2
N_GMLP_LAYERS = DEPTH // 2
N_HEADS = 16
HEAD_DIM = D_MODEL // N_HEADS
NA_ROWS = 8
NA_COLS = 16
Q_BLOCK = 128
CHUNK = 128
GMLP_WIDTH = 2 * D_MODEL
GMLP_GROUPS = 16
GMLP_GROUP_DIM = GMLP_WIDTH // GMLP_GROUPS
FFN_DIM = 2816
CONV_W = 3
EPS = 1e-6

kernel_name = 'hybrid_natten_gmlp_diffusion_step'


def rmsnorm(x, g):
    xf = x.astype(jnp.float32)
    y = xf * lax.rsqrt(jnp.mean(xf * xf, axis=-1, keepdims=True) + EPS)
    return (y * g.astype(jnp.float32)).astype(x.dtype)


def layernorm(x, g):
    xf = x.astype(jnp.float32)
    xc = xf - jnp.mean(xf, axis=-1, keepdims=True)
    y = xc * lax.rsqrt(jnp.mean(xc * xc, axis=-1, keepdims=True) + EPS)
    return (y * g.astype(jnp.float32)).astype(x.dtype)


def adaln(cond, w_ada, b_ada):
    mod = jax.nn.silu(cond) @ w_ada + b_ada
    return jnp.split(mod[:, None, :], 6, axis=-1)


def modulate(h, shift, scale):
    return h * (1 + scale) + shift


def qkv_proj(h, w_qkv):
    B, T, _ = h.shape
    qkv = (h @ w_qkv).reshape(B, T, 3, N_HEADS, HEAD_DIM)
    qkv = jnp.transpose(qkv, (2, 0, 3, 1, 4))
    return qkv[0], qkv[1], qkv[2]


def merge_heads(o):
    B, H, T, Dh = o.shape
    return jnp.transpose(o, (0, 2, 1, 3)).reshape(B, T, H * Dh)


def context_attention(q, k, v):
    B, H, S, Dh = q.shape
    nb = S // Q_BLOCK
    scale = HEAD_DIM ** -0.5
    qb = jnp.moveaxis(q.reshape(B, H, nb, Q_BLOCK, Dh), 2, 0)

    def block(qi):
        s = jnp.einsum('bhqd,bhkd->bhqk', qi, k).astype(jnp.float32) * scale
        p = jax.nn.softmax(s, axis=-1).astype(v.dtype)
        return jnp.einsum('bhqk,bhkd->bhqd', p, v)

    o = lax.map(block, qb)
    return jnp.moveaxis(o, 0, 2).reshape(B, H, S, Dh)


def neighbourhood_attention(q, k, v, k_ctx, v_ctx, rpb):
    B, H, T, Dh = q.shape
    rows = T // GRID_W
    kh = min(NA_ROWS, rows)
    n_win = kh * NA_COLS
    scale = HEAD_DIM ** -0.5
    qg = q.reshape(B, H, rows, GRID_W, Dh)
    kg = k.reshape(B, H, rows, GRID_W, Dh)
    vg = v.reshape(B, H, rows, GRID_W, Dh)
    r_idx = jnp.arange(rows)
    row_start = jnp.clip(r_idx - kh // 2, 0, rows - kh)
    cols = jnp.arange(GRID_W)
    col_start = jnp.clip(cols - NA_COLS // 2, 0, GRID_W - NA_COLS)
    col_idx = col_start[:, None] + jnp.arange(NA_COLS)[None, :]
    col_off = col_idx - cols[:, None] + (NA_COLS - 1)

    def row_block(args):
        q_r, r, rs = args
        k_blk = lax.dynamic_slice_in_dim(kg, rs, kh, axis=2)
        v_blk = lax.dynamic_slice_in_dim(vg, rs, kh, axis=2)
        k_win = k_blk[:, :, :, col_idx, :]
        v_win = v_blk[:, :, :, col_idx, :]
        row_off = rs + jnp.arange(kh) - r + (NA_ROWS - 1)
        bias = jnp.transpose(rpb[:, row_off][:, :, col_off], (0, 2, 1, 3))
        s_win = (jnp.einsum('bhqd,bhiqjd->bhqij', q_r, k_win).astype(jnp.float32) * scale
                 + bias.astype(jnp.float32)).reshape(B, H, GRID_W, n_win)
        s_ctx = jnp.einsum('bhqd,bhcd->bhqc', q_r, k_ctx).astype(jnp.float32) * scale
        p = jax.nn.softmax(jnp.concatenate([s_win, s_ctx], axis=-1), axis=-1).astype(v.dtype)
        p_win = p[..., :n_win].reshape(B, H, GRID_W, kh, NA_COLS)
        p_ctx = p[..., n_win:]
        return (jnp.einsum('bhqij,bhiqjd->bhqd', p_win, v_win)
                + jnp.einsum('bhqc,bhcd->bhqd', p_ctx, v_ctx))

    o = lax.map(row_block, (jnp.moveaxis(qg, 2, 0), r_idx, row_start))
    return jnp.moveaxis(o, 0, 2).reshape(B, H, T, Dh)


def chunk_gmlp(h, w_in, g_v, w_s, b_s, w_out):
    B, T, _ = h.shape
    nc = T // CHUNK
    z = jax.nn.gelu(h @ w_in)
    u, vv = jnp.split(z, 2, axis=-1)
    vv = layernorm(vv, g_v).reshape(B, nc, CHUNK, GMLP_GROUPS, GMLP_GROUP_DIM)
    mixed = jnp.einsum('gpq,bnqgc->bnpgc', w_s, vv) + jnp.transpose(b_s)[:, :, None]
    return (u * mixed.reshape(B, T, GMLP_WIDTH)) @ w_out


def conv_ffn(h, w_up, w_conv, b_conv, w_down):
    T = h.shape[1]
    a = h @ w_up
    pad = CONV_W // 2
    ap = jnp.pad(a, ((0, 0), (pad, pad), (0, 0)))
    a = sum(ap[:, j:j + T] * w_conv[j] for j in range(CONV_W)) + b_conv
    g, val = jnp.split(a, 2, axis=-1)
    return (jax.nn.silu(g) * val) @ w_down


def setup_inputs(seed: int = 0) -> dict:
    key = jax.random.key(seed)
    ks = jax.random.split(key, 24)
    D, E, F, G = D_MODEL, GMLP_WIDTH, FFN_DIM, GMLP_GROUPS
    nrm = jax.random.normal
    return {
        'x_prompt': nrm(ks[0], (BATCH, SEQ, D), jnp.float32),
        'x_sample': nrm(ks[1], (DEC_BATCH, DEC_SEQ, D), jnp.float32),
        'cache_k': nrm(ks[2], (DEC_BATCH, N_ATTN_LAYERS, N_HEADS, PAST_LEN, HEAD_DIM), jnp.float32),
        'cache_v': nrm(ks[3], (DEC_BATCH, N_ATTN_LAYERS, N_HEADS, PAST_LEN, HEAD_DIM), jnp.float32),
        'c': nrm(ks[4], (DEC_BATCH, D), jnp.float32),
        'c_ctx': nrm(ks[5], (D,), jnp.float32),
        'w_ada': nrm(ks[6], (DEPTH, D, 6 * D), jnp.float32) * D ** -0.5,
        'b_ada': nrm(ks[7], (DEPTH, 6 * D), jnp.float32) * 0.01,
        'norm_mix_g': 1.0 + 0.02 * nrm(ks[8], (DEPTH, D), jnp.float32),
        'norm_ffn_g': 1.0 + 0.02 * nrm(ks[9], (DEPTH, D), jnp.float32),
        'norm_final_g': 1.0 + 0.02 * nrm(ks[10], (D,), jnp.float32),
        'w_qkv': nrm(ks[11], (N_ATTN_LAYERS, D, 3 * D), jnp.float32) * D ** -0.5,
        'w_attn_out': nrm(ks[12], (N_ATTN_LAYERS, D, D), jnp.float32) * D ** -0.5,
        'rpb': 0.2 * nrm(ks[13], (N_ATTN_LAYERS, N_HEADS, 2 * NA_ROWS - 1, 2 * NA_COLS - 1), jnp.float32),
        'w_gmlp_in': nrm(ks[14], (N_GMLP_LAYERS, D, 2 * E), jnp.float32) * D ** -0.5,
        'g_gmlp_v': 1.0 + 0.02 * nrm(ks[15], (N_GMLP_LAYERS, E), jnp.float32),
        'w_spatial': nrm(ks[16], (N_GMLP_LAYERS, G, CHUNK, CHUNK), jnp.float32) * CHUNK ** -0.5,
        'b_spatial': 1.0 + 0.01 * nrm(ks[17], (N_GMLP_LAYERS, G, CHUNK), jnp.float32),
        'w_gmlp_out': nrm(ks[18], (N_GMLP_LAYERS, E, D), jnp.float32) * E ** -0.5,
        'w_ffn_up': nrm(ks[19], (DEPTH, D, 2 * F), jnp.float32) * D ** -0.5,
        'w_ffn_conv': nrm(ks[20], (DEPTH, CONV_W, 2 * F), jnp.float32) * CONV_W ** -0.5,
        'b_ffn_conv': 0.01 * nrm(ks[21], (DEPTH, 2 * F), jnp.float32),
        'w_ffn_down': nrm(ks[22], (DEPTH, F, D), jnp.float32) * F ** -0.5,
    }


def reference(x_prompt, x_sample, cache_k, cache_v, c, c_ctx, w_ada, b_ada, norm_mix_g,
              norm_ffn_g, norm_final_g, w_qkv, w_attn_out, rpb, w_gmlp_in, g_gmlp_v,
              w_spatial, b_spatial, w_gmlp_out, w_ffn_up, w_ffn_conv, b_ffn_conv, w_ffn_down):
    x = x_prompt
    ks_new, vs_new = [], []
    for l in range(DEPTH):
        sm, cm, gm, sf, cf, gf = adaln(c_ctx[None, :], w_ada[l], b_ada[l])
        h = modulate(rmsnorm(x, norm_mix_g[l]), sm, cm)
        j = l // N_MIXERS
        if l % N_MIXERS == 0:
            q, k, v = qkv_proj(h, w_qkv[j])
            mix = merge_heads(context_attention(q, k, v)) @ w_attn_out[j]
            ks_new.append(k)
            vs_new.append(v)
        else:
            mix = chunk_gmlp(h, w_gmlp_in[j], g_gmlp_v[j], w_spatial[j], b_spatial[j], w_gmlp_out[j])
        x = x + gm * mix
        h = modulate(rmsnorm(x, norm_ffn_g[l]), sf, cf)
        x = x + gf * conv_ffn(h, w_ffn_up[l], w_ffn_conv[l], b_ffn_conv[l], w_ffn_down[l])
    y_prompt = rmsnorm(x, norm_final_g)
    state_k = jnp.stack(ks_new, axis=1)
    state_v = jnp.stack(vs_new, axis=1)

    x = x_sample
    for l in range(DEPTH):
        sm, cm, gm, sf, cf, gf = adaln(c, w_ada[l], b_ada[l])
        h = modulate(rmsnorm(x, norm_mix_g[l]), sm, cm)
        j = l // N_MIXERS
        if l % N_MIXERS == 0:
            q, k, v = qkv_proj(h, w_qkv[j])
            o = neighbourhood_attention(q, k, v, cache_k[:, j], cache_v[:, j], rpb[j])
            mix = merge_heads(o) @ w_attn_out[j]
        else:
            mix = chunk_gmlp(h, w_gmlp_in[j], g_gmlp_v[j], w_spatial[j], b_spatial[j], w_gmlp_out[j])
        x = x + gm * mix
        h = modulate(rmsnorm(x, norm_ffn_g[l]), sf, cf)
        x = x + gf * conv_ffn(h, w_ffn_up[l], w_ffn_conv[l], b_ffn_conv[l], w_ffn_down[l])
    y_sample = rmsnorm(x, norm_final_g)

    return (y_prompt, y_sample, state_k, state_v)
```

```python
import contextlib
import numpy as np
import concourse.bass as bass
import concourse.mybir as mybir
from concourse.bass_utils import run_bass_kernel_spmd

F32 = mybir.dt.float32
BF16 = mybir.dt.bfloat16
AF = mybir.ActivationFunctionType
ALU = mybir.AluOpType

D = 1024
NH = 16
FF = 2816
EE = 2048
EPS = 1e-6
NEG = -30000.0
ENGS = ("pe", "act", "dve", "pool", "sp")

R_BADA = 0
R_CCTX = 96
R_C = 104
R_GMIX = 112
R_GFFN = 128
R_GFIN = 144
R_WC = 152
R_BC = 416
N_ROWS = 512


class Tile:
    __slots__ = ("name", "w", "r", "excl", "strict")

    def __init__(self, name, excl=False, strict=False):
        self.name = name
        self.w = {}
        self.r = {}
        self.excl = excl
        self.strict = strict


class Prog:
    def __init__(self, nc, stack, n_dma_sems=16):
        self.nc = nc
        self.sems = {e: stack.enter_context(nc.semaphore("sem_" + e)) for e in ENGS}
        self.cnt = {e: 0 for e in ENGS}
        self.lists = {e: [] for e in ENGS}
        self.waited = {e: {} for e in ENGS}
        self.dma_sems = {}
        self.dma_rr = {}
        self.dma_val = {}
        self.guard = None
        for q in ("sp", "pool"):
            self.dma_sems[q] = [stack.enter_context(nc.semaphore(f"dsem_{q}_{i}"))
                                for i in range(n_dma_sems)]
            self.dma_rr[q] = 0

    def _sem(self, key):
        if isinstance(key, str):
            return self.sems[key]
        return self.dma_sems[key[0]][key[1]]

    def _wait(self, e, key, val):
        if self.waited[e].get(key, 0) >= val:
            return
        self.waited[e][key] = val
        self.lists[e].append(("wait", key, val))

    def _deps(self, e, reads, writes, is_dma=False):
        for t in reads:
            for key, (val, src) in t.w.items():
                if src == e and e == "pe" and not is_dma:
                    continue
                self._wait(e, key, val)
            if t.excl:
                for key, (val, src) in t.r.items():
                    if src != e:
                        self._wait(e, key, val)
        for t in writes:
            for key, (val, src) in t.w.items():
                if src == e and e == "pe" and not is_dma:
                    continue
                self._wait(e, key, val)
            for key, (val, src) in t.r.items():
                if src == e and e == "pe" and not is_dma:
                    continue
                self._wait(e, key, val)

    def snapshot(self):
        g = Tile("guard")
        for f in ENGS:
            if self.cnt[f]:
                g.w[f] = (self.cnt[f], f)
        for key, val in self.dma_val.items():
            g.w[key] = (val, "dma")
        self.guard = g

    def barrier(self, snapshot=True):
        if snapshot:
            self.snapshot()
        for e in ENGS:
            if e in ("pool", "pe"):
                continue
            for f in ENGS:
                if f != e and self.cnt[f]:
                    self._wait(e, f, self.cnt[f])
            for key, val in self.dma_val.items():
                self._wait(e, key, val)

    @staticmethod
    def _mark(key, val, src, reads, writes):
        for t in reads:
            t.r[key] = (val, src)
        for t in writes:
            t.w[key] = (val, src)

    def op(self, e, fns, reads=(), writes=(), guard=False):
        if callable(fns):
            fns = [fns]
        if guard and self.guard is not None:
            self._deps(e, (), [self.guard], is_dma=True)
        self._deps(e, reads, writes)
        self.cnt[e] += 1
        self.lists[e].append(("op", fns))
        self._mark(e, self.cnt[e], e, reads, writes)

    def dma(self, q, fn, reads=(), writes=(), guard=False):
        if guard and self.guard is not None:
            self._deps(q, (), [self.guard], is_dma=True)
        self._deps(q, reads, writes, is_dma=True)
        i = self.dma_rr[q]
        self.dma_rr[q] = (i + 1) % len(self.dma_sems[q])
        key = (q, i)
        prev = self.dma_val.get(key, 0)
        if prev:
            self._wait(q, key, prev)
        val = prev + 16
        self.dma_val[key] = val
        self.lists[q].append(("dma", fn, key))
        self._mark(key, val, "dma", reads, writes)

    def finish(self):
        for key, val in self.dma_val.items():
            self._wait("sp", key, val)

    def emit(self, block):
        prog = self

        def run(e, eng):
            for item in prog.lists[e]:
                if item[0] == "wait":
                    eng.wait_ge(prog._sem(item[1]), item[2])
                elif item[0] == "op":
                    ins = None
                    for f in item[1]:
                        ins = f(eng)
                    ins.then_inc(prog.sems[e], 1)
                else:
                    item[1](eng).then_inc(prog._sem(item[2]), 16)

        @block.sync
        def _(eng):
            run("sp", eng)

        @block.scalar
        def _(eng):
            run("act", eng)

        @block.vector
        def _(eng):
            run("dve", eng)

        @block.gpsimd
        def _(eng):
            run("pool", eng)

        @block.tensor
        def _(eng):
            run("pe", eng)


class _Stop(Exception):
    pass


def build_program(do_prompt=True, do_sample=True, stop=None):
    nc = bass.Bass("TRN2", target_bir_lowering=False)

    def chk(name):
        if stop == name:
            raise _Stop()

    def din(name, shape):
        return nc.dram_tensor(name, list(shape), F32, kind="ExternalInput")

    def dout(name, shape):
        return nc.dram_tensor(name, list(shape), F32, kind="ExternalOutput")

    d_xp = din("xp", [1024, D]).ap()
    d_xs = din("xs", [1024, D]).ap()
    d_ck = din("ck", [NH, 512, 64]).ap()
    d_cv = din("cv", [NH, 512, 64]).ap()
    d_rows = din("rows", [N_ROWS, 128]).ap()
    d_ident = din("ident", [128, 128]).ap()
    d_jmat = din("jmat", [128, 128]).ap()
    d_mint = din("mint", [128, 1280]).ap()
    d_mtop = din("mtop", [128, 1280]).ap()
    t_wada = din("w_ada", [2, D, 6 * D])
    t_bada = din("b_ada", [2, 6 * D])
    d_wqkv = din("w_qkv", [D, 3 * D]).ap()
    d_wo = din("w_o", [D, D]).ap()
    d_rpb = din("rpb", [NH * 15, 31]).ap()
    d_wgi = din("w_gi", [D, 2 * EE]).ap()
    t_gv = din("g_v", [EE])
    d_wsp = din("w_sp", [NH, 128, 128]).ap()
    d_bsp = din("b_sp", [NH, 128]).ap()
    d_wgo = din("w_go", [EE, D]).ap()
    t_wup = din("w_up", [2, D, 2 * FF])
    t_wdn = din("w_dn", [2, FF, D])
    t_gfin = din("g_fin", [D])
    t_rp = nc.dram_tensor("rp_scratch", [NH * 15, 160], F32)
    d_yp = dout("yp", [1024, D]).ap()
    d_ys = dout("ys", [512, D]).ap()
    d_sk = dout("sk", [4, NH, 256, 64]).ap()
    d_sv = dout("sv", [4, NH, 256, 64]).ap()
    d_wada = t_wada.ap()
    d_wup = t_wup.ap()
    d_wdn = t_wdn.ap()

    with contextlib.ExitStack() as st:
        P = Prog(nc, st)

        def sb(name, shape, dt):
            return st.enter_context(nc.sbuf_tensor(name, list(shape), dt))

        X = sb("X", [128, 8, D], F32)
        tX = [Tile(f"X{t}") for t in range(8)]
        HT = sb("HT", [128, 8, 1024], BF16)
        tHT = [[Tile(f"HT{b}_{c}") for c in range(8)] for b in range(2)]
        WP = sb("WP", [128, 4, 8, 512], BF16)
        tWP = [Tile(f"WP{i}") for i in range(4)]
        GT = sb("GT", [128, 4, D], F32)
        tGT = [Tile(f"GT{i}") for i in range(4)]
        CT = sb("CT", [128, N_ROWS], F32)
        tCT = Tile("CT")
        MOD = sb("MOD", [128, 2, 48, 2], F32)
        tMOD = Tile("MOD")
        AM = sb("AM", [128, 2, 2, 8, 2], F32)
        tAM = Tile("AM")
        IDF = sb("IDF", [128, 128], F32)
        IDB = sb("IDB", [128, 128], BF16)
        JB = sb("JB", [128, 128], BF16)
        tID = Tile("ID")
        SC = sb("SC", [128, 16], BF16)
        tSC = Tile("SC")
        SS = sb("SS", [128, 8], F32)
        RS = sb("RS", [128, 8], F32)
        tSS = Tile("SS")
        tRS = Tile("RS")
        XS = sb("XS", [128, 2, D], BF16)
        tXS = [Tile("XS0"), Tile("XS1")]
        TMP = sb("TMP", [128, 2, 512], F32)
        tTMP = [Tile("TMP0"), Tile("TMP1")]
        ARENA_B = 95232
        AR = sb("AR", [128, ARENA_B // 2], BF16)

        def carve(off, nbytes):
            assert off % 4 == 0 and off + nbytes <= ARENA_B, (off, nbytes)
            return AR[:, off // 2:(off + nbytes) // 2]

        banks = [st.enter_context(nc.psum_tensor(f"bank{i}", [128, 512], F32)) for i in range(8)]
        tB = [Tile(f"bank{i}", excl=True) for i in range(8)]
        bank_rr = [0]

        def nb():
            i = bank_rr[0]
            bank_rr[0] = (i + 1) % 8
            return i

        wp_rr = [0]

        def next_wp():
            i = wp_rr[0]
            wp_rr[0] = (i + 1) % 3
            return i

        def load_panel(src2d, col0, ncols, row0=0, nrows=1024, wi=None):
            if wi is None:
                wi = next_wp()
            nk = nrows // 128
            src = src2d[row0:row0 + nrows, col0:col0 + ncols].rearrange("(k p) n -> p k n", p=128)
            P.dma("pool", lambda e: e.dma_start(out=WP[:, wi, 0:nk, 0:ncols], in_=src),
                  writes=[tWP[wi]])
            return wi

        _t3 = [Tile("IDF"), Tile("IDBt"), Tile("JBt")]
        P.dma("sp", lambda e: e.dma_start(out=IDF[:], in_=d_ident), writes=[_t3[0]])
        P.dma("pool", lambda e: e.dma_start(out=IDB[:], in_=d_ident), writes=[_t3[1]])
        P.dma("pool", lambda e: e.dma_start(out=JB[:], in_=d_jmat), writes=[_t3[2]])
        for _t in _t3:
            tID.w.update(_t.w)
        RW = carve(0, 4 * 128 * 4).bitcast(F32).rearrange("p (b c) -> p b c", c=128)
        tRW = Tile("RW")
        P.dma("sp", lambda e: e.dma_start(out=RW, in_=d_rows.rearrange("(b p) c -> p b c", p=128)),
              writes=[tRW])
        for n in range(4):
            src0 = d_wada[0][:, 512 * n:512 * n + 512].rearrange("(k p) n -> p k n", p=128)
            P.dma("pool", lambda e, n=n, src0=src0: e.dma_start(out=WP[:, n, :, :], in_=src0),
                  writes=[tWP[n]])
        bi = nb()
        P.op("pe", [lambda e, b=b: e.transpose(out=banks[bi][:, 128 * b:128 * b + 128],
                                               in_=RW[:, b, :], identity=IDF[:])
                    for b in range(4)], reads=[tRW, tID], writes=[tB[bi]])
        P.op("dve", lambda e: e.tensor_copy(out=CT[:], in_=banks[bi][:]), reads=[tB[bi]], writes=[tCT])
        P.op("act", lambda e: e.activation(out=SC[:], in_=CT[:, R_CCTX:R_CCTX + 16], func=AF.Silu),
             reads=[tCT], writes=[tSC])
        SCv = SC[:].rearrange("p (c k) -> p k c", c=2)
        deferred = []
        pending = [None]
        db_rr = [0]
        tDB = [Tile("DB0"), Tile("DB1")]

        def db_load(src2d, col0):
            i = db_rr[0]
            db_rr[0] = 1 - i
            src = src2d[:, col0:col0 + 256].rearrange("(k p) n -> p k n", p=128)
            P.dma("pool", lambda e: e.dma_start(out=WP[:, 3, :, 256 * i:256 * i + 256], in_=src),
                  writes=[tDB[i], tWP[3]])
            return i

        def run_deferred(n=1):
            for _ in range(n):
                if pending[0] is None and deferred:
                    it = deferred.pop(0)
                    pending[0] = (it, it[0]())
                if pending[0] is None:
                    return
                cur, cur_buf = pending[0]
                if deferred:
                    nxt = deferred.pop(0)
                    pending[0] = (nxt, nxt[0]())
                else:
                    pending[0] = None
                cur[1](cur_buf)

        def flush_deferred():
            while deferred or pending[0] is not None:
                run_deferred(1)

        def ada_item(l, n2):
            def load():
                return db_load(d_wada[l], 256 * n2)

            def compute(i):
                bm = nb()
                fns = []
                for oc in range(2):
                    for k in range(8):
                        fns.append(lambda e, oc=oc, k=k: e.matmul(
                            banks[bm][:, 2 * oc:2 * oc + 2], lhsT=WP[:, 3, k, 256 * i + 128 * oc:256 * i + 128 * oc + 128],
                            rhs=SCv[:, k, :], start=(k == 0), stop=(k == 7)))
                P.op("pe", fns, reads=[tDB[i], tSC], writes=[tB[bm]])
                r0 = R_BADA + 48 * l + 2 * n2
                for q_ in range(2):
                    P.op("act", lambda e, q_=q_: e.activation(
                        out=MOD[:, l, 2 * n2 + q_, :], in_=banks[bm][:, 2 * q_:2 * q_ + 2], func=AF.Identity,
                        bias=CT[:, r0 + q_:r0 + q_ + 1]), reads=[tB[bm], tCT], writes=[tMOD])
            return (load, compute)

        def am_item(l, wh):
            def compute(_):
                j = 1 if wh == 0 else 4
                g0 = (R_GMIX if wh == 0 else R_GFFN) + 8 * l
                P.op("dve", lambda e: e.tensor_scalar(
                    out=AM[:, l, wh, :, :], in0=MOD[:, l, 8 * j:8 * j + 8, :], scalar1=1.0, scalar2=None,
                    op0=ALU.add), reads=[tMOD], writes=[tAM])
                P.op("dve", lambda e: e.tensor_tensor(
                    out=AM[:, l, wh, :, :], in0=AM[:, l, wh, :, :],
                    in1=CT[:, g0:g0 + 8].unsqueeze(2).to_broadcast([128, 8, 2]), op=ALU.mult),
                    reads=[tAM, tCT], writes=[tAM])
            return (lambda: None, compute)

        for n in range(4):
            bm0 = nb()
            P.op("pe", [lambda e, n=n, oc=oc, k=k, bm0=bm0: e.matmul(
                banks[bm0][:, 2 * oc:2 * oc + 2], lhsT=WP[:, n, k, 128 * oc:128 * oc + 128],
                rhs=SCv[:, k, :], start=(k == 0), stop=(k == 7)) for oc in range(4) for k in range(8)],
                reads=[tWP[n], tSC], writes=[tB[bm0]])
            for q_ in range(4):
                P.op("act", lambda e, n=n, q_=q_, bm0=bm0: e.activation(
                    out=MOD[:, 0, 4 * n + q_, :], in_=banks[bm0][:, 2 * q_:2 * q_ + 2], func=AF.Identity,
                    bias=CT[:, R_BADA + 4 * n + q_:R_BADA + 4 * n + q_ + 1]), reads=[tB[bm0], tCT], writes=[tMOD])
        am_item(0, 0)[1](None)
        def defer_layer1():
            for n2 in range(24):
                deferred.append(ada_item(1, n2))
            deferred.append(am_item(1, 0))
            deferred.append(am_item(1, 1))

        SCB = XS[:, 1, :].rearrange("p (k c) -> p k c", c=128)

        def defer_gates(cond):
            def first(_):
                P.op("dve", lambda e: e.tensor_copy(
                    out=SCB, in_=SC[:, 8 * cond:8 * cond + 8].unsqueeze(2).to_broadcast([128, 8, 128])),
                    reads=[tSC], writes=[tXS[1]])
                for gi in range(4):
                    l, wh = gi // 2, gi % 2
                    c0 = 2048 if wh == 0 else 5120
                    src = bass.AP(t_bada, l * 6 * D + c0, [[0, 128], [1, D]])
                    P.dma("sp", lambda e, gi=gi, src=src: e.dma_start(out=GT[:, gi, :], in_=src),
                          writes=[tGT[gi]])

            def gate_item(gi, qq):
                l, wh = gi // 2, gi % 2
                c0 = (2048 if wh == 0 else 5120) + 256 * qq

                def load():
                    return db_load(d_wada[l], c0)

                def compute(i):
                    bk = nb()
                    P.op("pe", [lambda e, k=k: e.matmul(
                        banks[bk][:, 0:256], lhsT=SCB[:, k, :], rhs=WP[:, 3, k, 256 * i:256 * i + 256],
                        start=(k == 0), stop=(k == 7)) for k in range(8)],
                        reads=[tXS[1], tDB[i]], writes=[tB[bk]])
                    P.op("dve", lambda e: e.tensor_tensor(
                        out=GT[:, gi, 256 * qq:256 * qq + 256], in0=banks[bk][:, 0:256],
                        in1=GT[:, gi, 256 * qq:256 * qq + 256], op=ALU.add),
                        reads=[tB[bk], tGT[gi]], writes=[tGT[gi]])
                return (load, compute)

            deferred.append((lambda: None, first))
            for gi in range(4):
                for qq in range(4):
                    deferred.append(gate_item(gi, qq))

        def gate_riders(cond):
            def first():
                P.op("dve", lambda e: e.tensor_copy(
                    out=SCB, in_=SC[:, 8 * cond:8 * cond + 8].unsqueeze(2).to_broadcast([128, 8, 128])),
                    reads=[tSC], writes=[tXS[1]])
                for gi in range(4):
                    l, wh = gi // 2, gi % 2
                    c0 = 2048 if wh == 0 else 5120
                    src = bass.AP(t_bada, l * 6 * D + c0, [[0, 128], [1, D]])
                    P.dma("sp", lambda e, gi=gi, src=src: e.dma_start(out=GT[:, gi, :], in_=src),
                          writes=[tGT[gi]])

            def item(j):
                gi, hh = j // 2, j % 2
                l, wh = gi // 2, gi % 2
                c0 = (2048 if wh == 0 else 5120) + 512 * hh

                def compute(buf):
                    bk = nb()
                    P.op("pe", [lambda e, k=k: e.matmul(
                        banks[bk][:], lhsT=SCB[:, k, :], rhs=WP[:, buf, k, :],
                        start=(k == 0), stop=(k == 7)) for k in range(8)],
                        reads=[tXS[1], tWP[buf]], writes=[tB[bk]])
                    P.op("dve", lambda e: e.tensor_tensor(
                        out=GT[:, gi, 512 * hh:512 * hh + 512], in0=banks[bk][:],
                        in1=GT[:, gi, 512 * hh:512 * hh + 512], op=ALU.add),
                        reads=[tB[bk], tGT[gi]], writes=[tGT[gi]])
                return (d_wada[l], c0, compute)
            return first, [item(j) for j in range(8)]

        def ada_rider(l, n):
            def compute(buf):
                bmj = nb()
                P.op("pe", [lambda e, oc=oc, k=k: e.matmul(
                    banks[bmj][:, 2 * oc:2 * oc + 2], lhsT=WP[:, buf, k, 128 * oc:128 * oc + 128],
                    rhs=SCv[:, k, :], start=(k == 0), stop=(k == 7)) for oc in range(4) for k in range(8)],
                    reads=[tWP[buf], tSC], writes=[tB[bmj]])
                for q_ in range(4):
                    r_ = R_BADA + 48 * l + 4 * n + q_
                    P.op("act", lambda e, q_=q_, r_=r_: e.activation(
                        out=MOD[:, l, 4 * n + q_, :], in_=banks[bmj][:, 2 * q_:2 * q_ + 2], func=AF.Identity,
                        bias=CT[:, r_:r_ + 1]), reads=[tB[bmj], tCT], writes=[tMOD])
            return (d_wada[l], 512 * n, compute)

        def start_riders(riders, nbuf=3):
            st_ = {"next": 0}

            def load(j):
                if j < len(riders):
                    src2d, col0, _ = riders[j]
                    srcj = src2d[:, col0:col0 + 512].rearrange("(k p) n -> p k n", p=128)
                    bi_ = j % nbuf
                    P.dma("pool", lambda e: e.dma_start(out=WP[:, bi_, :, :], in_=srcj),
                          writes=[tWP[bi_]] + (tDB if bi_ == 3 else []))

            def step():
                j = st_["next"]
                if j < len(riders):
                    load(j + nbuf - 1)
                    riders[j][2](j % nbuf)
                    st_["next"] = j + 1

            for j0 in range(nbuf - 1):
                load(j0)
            return step, st_

        def rstd_from_ss(n, scale, tin, SSap, RSap, tout):
            P.op("dve", lambda e: e.tensor_scalar(out=RSap, in0=SSap, scalar1=scale, scalar2=EPS,
                                                  op0=ALU.mult, op1=ALU.add), reads=[tin], writes=[tout])
            P.op("act", lambda e: e.activation(out=RSap, in_=RSap, func=AF.Sqrt), reads=[tout], writes=[tout])
            P.op("dve", lambda e: e.reciprocal(out=RSap, in_=RSap), reads=[tout], writes=[tout])

        tJ = Tile("junk", strict=True)

        def sumsq(ntiles):
            for tb in range(0, ntiles, 4):
                te = min(ntiles, tb + 4)
                for t in range(tb, te):
                    P.op("act", lambda e, t=t: e.activation(out=XS[:, 0, :], in_=X[:, t, :], func=AF.Square,
                                                            accum_out=SS[:, t:t + 1]),
                         reads=[tX[t]], writes=[tXS[0], tJ, tSS])
                rstd_from_ss(te - tb, 1.0 / D, tSS, SS[:, tb:te], RS[:, tb:te], tRS)

        def norm_mod(l, wh, ntiles, cond):
            jsh = 0 if wh == 0 else 3
            blks = [(tb, min(ntiles, tb + 4)) for tb in range(0, ntiles, 4)]

            def squares(tb, te):
                for t in range(tb, te):
                    P.op("act", lambda e, t=t: e.activation(out=XS[:, 0, :], in_=X[:, t, :], func=AF.Square,
                                                            accum_out=SS[:, t:t + 1]),
                         reads=[tX[t]], writes=[tXS[0], tJ, tSS])

            def scale_T(tb, te):
                bks = [nb() for _ in range(4)]
                for t in range(tb, te):
                    xi = 1 if wh_xs[0] else t % 2
                    P.op("dve", lambda e, t=t, xi=xi: e.tensor_scalar(
                        out=XS[:, xi, :], in0=X[:, t, :], scalar1=RS[:, t:t + 1], scalar2=None,
                        op0=ALU.mult), reads=[tX[t], tRS], writes=[tXS[xi]])
                    P.op("pe", [lambda e, c=c, t=t, xi=xi: e.transpose(
                        out=banks[bks[c // 2]][:].bitcast(BF16)[:, (c % 2) * 512 + (t - tb) * 128:
                                                                 (c % 2) * 512 + (t - tb) * 128 + 128],
                        in_=XS[:, xi, 128 * c:128 * c + 128], identity=IDB[:]) for c in range(8)],
                        reads=[tXS[xi], tID], writes=[tB[b_] for b_ in bks])
                return bks

            def evac(tb, te, bks):
                ntb = te - tb
                for c in range(8):
                    P.op("act", lambda e, c=c: e.activation(
                        out=HT[:, c, 128 * tb:128 * te],
                        in_=banks[bks[c // 2]][:].bitcast(BF16)[:, (c % 2) * 512:(c % 2) * 512 + 128 * ntb],
                        func=AF.Identity, scale=AM[:, l, wh, c, cond:cond + 1],
                        bias=MOD[:, l, 8 * jsh + c, cond:cond + 1]),
                        reads=[tB[bks[c // 2]], tAM, tMOD], writes=[tHT[tb // 4][c]])

            wh_xs = [len(blks) > 1]
            squares(*blks[0])
            rstd_from_ss(0, 1.0 / D, tSS, SS[:, blks[0][0]:blks[0][1]], RS[:, blks[0][0]:blks[0][1]], tRS)
            pend = None
            for bi_, (tb, te) in enumerate(blks):
                if bi_ + 1 < len(blks):
                    squares(*blks[bi_ + 1])
                else:
                    wh_xs[0] = False
                bks = scale_T(tb, te)
                if bi_ + 1 < len(blks):
                    nb0, nb1 = blks[bi_ + 1]
                    rstd_from_ss(0, 1.0 / D, tSS, SS[:, nb0:nb1], RS[:, nb0:nb1], tRS)
                if pend is not None:
                    evac(*pend)
                pend = (tb, te, bks)
            evac(*pend)

        def residual(bk, t, hh, gi, eng="dve"):
            ti = (2 * t + hh) % 2
            P.op("dve", lambda e: e.tensor_tensor(out=TMP[:, ti, :], in0=banks[bk][:],
                                                  in1=GT[:, gi, 512 * hh:512 * hh + 512], op=ALU.mult),
                 reads=[tB[bk], tGT[gi]], writes=[tTMP[ti]])
            P.op(eng, lambda e: e.tensor_tensor(out=X[:, t, 512 * hh:512 * hh + 512],
                                                   in0=X[:, t, 512 * hh:512 * hh + 512],
                                                   in1=TMP[:, ti, :], op=ALU.add),
                 reads=[tTMP[ti], tX[t]], writes=[tX[t]])

        def tok_blocks(ntok):
            out = []
            t0 = 0
            while t0 < ntok:
                out.append((t0, min(512, ntok - t0)))
                t0 += 512
            return out

        def ht_tiles(t0, n):
            return [tl for b in range(t0 // 512, (t0 + n - 1) // 512 + 1) for tl in tHT[b]]

        last_cv = [None]

        def ffn(l, cond, blocks, out_tiles, ntiles_norm):
            P.snapshot()
            norm_mod(l, 1, ntiles_norm, cond)
            P.barrier(snapshot=False)
            UT = carve(0, 32768).rearrange("p (b i t) -> p b i t", b=2, i=8)
            tUT = [Tile("UT0"), Tile("UT1")]
            WD = carve(32768, 32768).rearrange("p (b i n) -> p b i n", b=2, i=8)
            tWD = [Tile("WD0"), Tile("WD1")]
            CV = carve(65536, 6 * 4096).bitcast(F32).rearrange("p (s t) -> p s t", t=1024)
            tCV = [Tile(f"CV{i}") for i in range(6)]
            last_cv[0] = (CV, tCV)
            gi = 2 * l + 1
            pieces = [(0, 8), (8, 8), (16, 6)]
            wc0 = R_WC + (l * 3) * 44
            bc0 = R_BC + l * 44
            panels = [(c0 + pn, min(2, npc - pn)) for (c0, npc) in pieces for pn in range(0, npc, 2)]
            cidx = [0]

            def issue_panel(idx):
                if idx >= len(panels):
                    return
                cj, nch = panels[idx]
                wbuf = idx % 3
                for (co, src0) in ((0, 128 * cj), (256, FF + 128 * cj)):
                    srcw = d_wup[l][:, src0:src0 + 128 * nch].rearrange("(k p) n -> p k n", p=128)
                    P.dma("pool", lambda e, co=co, srcw=srcw: e.dma_start(
                        out=WP[:, wbuf, :, co:co + 128 * nch], in_=srcw), writes=[tWP[wbuf]])

            def issue_wd(pi):
                if pi >= len(pieces):
                    return
                c0_, npc_ = pieces[pi]
                src = d_wdn[l][128 * c0_:128 * (c0_ + npc_), :].rearrange("(i p) n -> p i n", p=128)
                P.dma("pool", lambda e: e.dma_start(out=WD[:, pi % 2, 0:npc_, :], in_=src),
                      writes=[tWD[pi % 2]], guard=True)

            def up_set(pi, j, jj, oc, wbuf, blk):
                (a0, a1, o0, o1, L) = blk
                ub = pi % 2
                na = a1 - a0
                bg, bv = nb(), nb()
                s_ = (cidx[0] % 2) * 3
                cidx[0] += 1
                for (bk, vo) in ((bg, 0), (bv, 256)):
                    P.op("pe", [lambda e, k=k, bk=bk, vo=vo: e.matmul(
                        banks[bk][:, 0:na], lhsT=WP[:, wbuf, k, vo + 128 * oc:vo + 128 * oc + 128],
                        rhs=HT[:, k, a0:a1], start=(k == 0), stop=(k == 7)) for k in range(8)],
                        reads=[tWP[wbuf]] + ht_tiles(a0, na), writes=[tB[bk]])
                for (bk, so, fo) in ((bg, 0, 0), (bv, 1, 22)):
                    w0 = CT[:, wc0 + fo + j:wc0 + fo + j + 1]
                    w1 = CT[:, wc0 + 44 + fo + j:wc0 + 44 + fo + j + 1]
                    w2 = CT[:, wc0 + 88 + fo + j:wc0 + 88 + fo + j + 1]
                    bb = CT[:, bc0 + fo + j:bc0 + fo + j + 1]
                    cv = s_ + so
                    P.op("act", lambda e, bk=bk, cv=cv, w1=w1, bb=bb: e.activation(
                        out=CV[:, cv, o0:o1], in_=banks[bk][:, o0 - a0:o1 - a0], func=AF.Identity,
                        scale=w1, bias=bb), reads=[tB[bk], tCT], writes=[tCV[cv]])
                    if L and (o1 - o0) % L == 0 and a0 == o0 and a1 == o1:
                        ov = CV[:, cv, o0:o1].rearrange("p (g t) -> p g t", t=L)
                        pv = banks[bk][:, 0:na].rearrange("p (g t) -> p g t", t=L)
                        P.op("dve", lambda e, ov=ov, pv=pv, w0=w0: e.scalar_tensor_tensor(
                            out=ov[:, :, 1:L], in0=pv[:, :, 0:L - 1], scalar=w0, in1=ov[:, :, 1:L],
                            op0=ALU.mult, op1=ALU.add), reads=[tB[bk], tCV[cv], tCT], writes=[tCV[cv]])
                        P.op("dve", lambda e, ov=ov, pv=pv, w2=w2: e.scalar_tensor_tensor(
                            out=ov[:, :, 0:L - 1], in0=pv[:, :, 1:L], scalar=w2, in1=ov[:, :, 0:L - 1],
                            op0=ALU.mult, op1=ALU.add), reads=[tB[bk], tCV[cv], tCT], writes=[tCV[cv]])
                    else:
                        l0 = max(o0, 1, a0 + 1)
                        P.op("dve", lambda e, bk=bk, cv=cv, w0=w0, l0=l0: e.scalar_tensor_tensor(
                            out=CV[:, cv, l0:o1], in0=banks[bk][:, l0 - 1 - a0:o1 - 1 - a0], scalar=w0,
                            in1=CV[:, cv, l0:o1], op0=ALU.mult, op1=ALU.add),
                            reads=[tB[bk], tCV[cv], tCT], writes=[tCV[cv]])
                        r1 = min(o1, a1 - 1)
                        P.op("dve", lambda e, bk=bk, cv=cv, w2=w2, r1=r1: e.scalar_tensor_tensor(
                            out=CV[:, cv, o0:r1], in0=banks[bk][:, o0 + 1 - a0:r1 + 1 - a0], scalar=w2,
                            in1=CV[:, cv, o0:r1], op0=ALU.mult, op1=ALU.add),
                            reads=[tB[bk], tCV[cv], tCT], writes=[tCV[cv]])
                P.op("act", lambda e: e.activation(out=CV[:, s_ + 2, o0:o1], in_=CV[:, s_, o0:o1], func=AF.Silu),
                     reads=[tCV[s_]], writes=[tCV[s_ + 2]])
                P.op("dve", lambda e: e.tensor_tensor(
                    out=UT[:, ub, jj, o0:o1], in0=CV[:, s_ + 2, o0:o1], in1=CV[:, s_ + 1, o0:o1], op=ALU.mult),
                    reads=[tCV[s_ + 2], tCV[s_ + 1]], writes=[tUT[ub]])

            def down_unit(pi, t, hh):
                npc_ = pieces[pi][1]
                bk = nb()
                P.op("pe", [lambda e, i=i: e.matmul(
                    banks[bk][:], lhsT=UT[:, pi % 2, i, 128 * t:128 * t + 128],
                    rhs=WD[:, pi % 2, i, 512 * hh:512 * hh + 512], start=(i == 0), stop=(i == npc_ - 1))
                    for i in range(npc_)], reads=[tUT[pi % 2], tWD[pi % 2]], writes=[tB[bk]])
                residual(bk, t, hh, gi)

            issue_panel(0)
            issue_panel(1)
            issue_wd(0)
            pidx = 0
            for pi, (c0, npc) in enumerate(pieces):
                sets = []
                for pn in range(0, npc, 2):
                    for oc in range(min(2, npc - pn)):
                        for blk in blocks:
                            sets.append((pn, oc, blk))
                downs = [(t, hh) for t in out_tiles for hh in range(2)] if pi > 0 else []
                dptr = 0
                last_pn = None
                for si, (pn, oc, blk) in enumerate(sets):
                    if pn != last_pn:
                        wbuf = pidx % 3
                        issue_panel(pidx + 2)
                        pidx += 1
                        last_pn = pn
                    run_deferred(1)
                    up_set(pi, c0 + pn + oc, pn + oc, oc, wbuf, blk)
                    want = ((si + 1) * len(downs)) // len(sets)
                    while dptr < want:
                        down_unit(pi - 1, *downs[dptr])
                        dptr += 1
                issue_wd(pi + 1)
            for t in out_tiles:
                for hh in range(2):
                    down_unit(len(pieces) - 1, t, hh)
            wp_rr[0] = pidx % 3

        def gmlp(cond, ntiles):
            ntok = 128 * ntiles
            P.snapshot()
            norm_mod(1, 0, ntiles, cond)
            P.barrier(snapshot=False)
            load_bsb()
            UT = carve(0, 32768).rearrange("p (g t) -> p g t", t=1024)
            tUT = [Tile(f"GU{b}") for b in range(8)]
            VV = carve(32768, 32768).rearrange("p (t f) -> p t f", f=2048)
            tVV = [Tile(f"VV{t}") for t in range(8)]
            WST = carve(65536, 4096).rearrange("p (g q) -> p g q", q=128)
            tWST = Tile("WST")
            GVB = carve(69632, 8192).bitcast(F32)
            tGVB = Tile("GVB")
            BSG = carve(77824, 512).bitcast(F32)
            tBSG = Tile("BSG")
            ST6 = carve(78336, 8 * 4 * 6 * 4).bitcast(F32).rearrange("p (t n s) -> p t n s", n=4, s=6)
            MV = carve(79104, 8 * 2 * 4).bitcast(F32).rearrange("p (t s) -> p t s", s=2)
            LR = carve(79168, 8 * 4).bitcast(F32)
            LB = carve(79200, 8 * 4).bitcast(F32)
            tST = Tile("ST")
            tLN = Tile("LN")
            P.dma("sp", lambda e: e.dma_start(out=GVB, in_=bass.AP(t_gv, 0, [[0, 128], [1, EE]])), writes=[tGVB])
            wi = next_wp()
            WSv = WP[:, wi, :, :].rearrange("p k n -> p (k n)").rearrange("p (g q) -> p g q", q=128)
            P.dma("pool", lambda e: e.dma_start(out=WSv[:, 0:16, :], in_=d_wsp.rearrange("g p q -> p g q")),
                  writes=[tWP[wi]])
            for half in range(2):
                bk = nb()
                P.op("pe", [lambda e, g=g, bk=bk, half=half: e.transpose(
                    out=banks[bk][:].bitcast(BF16)[:, 128 * g:128 * g + 128], in_=WSv[:, 8 * half + g, :],
                    identity=IDB[:]) for g in range(8)], reads=[tWP[wi], tID], writes=[tB[bk]])
                P.op("dve", lambda e, bk=bk, half=half: e.tensor_copy(
                    out=WST[:, 8 * half:8 * half + 8, :],
                    in_=banks[bk][:].bitcast(BF16).rearrange("p (g q) -> p g q", q=128)),
                    reads=[tB[bk]], writes=[tWST])
            for n in range(4):
                wv = load_panel(d_wgi, EE + 512 * n, 512)
                for t in range(ntiles):
                    bk = nb()
                    P.op("pe", [lambda e, k=k, t=t, bk=bk, wv=wv: e.matmul(
                        banks[bk][:], lhsT=HT[:, k, 128 * t:128 * t + 128], rhs=WP[:, wv, k, :],
                        start=(k == 0), stop=(k == 7)) for k in range(8)],
                        reads=[tWP[wv]] + ht_tiles(128 * t, 128), writes=[tB[bk]])
                    P.op("act", lambda e, t=t, n=n, bk=bk: e.activation(
                        out=VV[:, t, 512 * n:512 * n + 512], in_=banks[bk][:], func=AF.Gelu_apprx_tanh),
                        reads=[tB[bk]], writes=[tVV[t]])
                    P.op("dve", lambda e, t=t, n=n: e.bn_stats(out=ST6[:, t, n, :], in_=VV[:, t, 512 * n:512 * n + 512]),
                         reads=[tVV[t]], writes=[tST])
            for t in range(ntiles):
                P.op("dve", lambda e, t=t: e.bn_aggr(out=MV[:, t, :], in_=ST6[:, t, :, :].rearrange("p n s -> p (n s)")),
                     reads=[tST], writes=[tLN])
            rstd_from_ss(ntiles, 1.0, tLN, MV[:, 0:ntiles, 1], LR[:, 0:ntiles], tLN)
            P.op("dve", lambda e: e.scalar_tensor_tensor(out=LB[:, 0:ntiles], in0=MV[:, 0:ntiles, 0], scalar=-1.0,
                                                         in1=LR[:, 0:ntiles], op0=ALU.mult, op1=ALU.mult),
                 reads=[tLN], writes=[tLN])
            for t in range(ntiles):
                P.op("dve", lambda e, t=t: e.tensor_scalar(out=VV[:, t, :], in0=VV[:, t, :], scalar1=LR[:, t:t + 1],
                                                           scalar2=LB[:, t:t + 1], op0=ALU.mult, op1=ALU.add),
                     reads=[tVV[t], tLN], writes=[tVV[t]])
                P.op("dve", lambda e, t=t: e.tensor_tensor(out=VV[:, t, :], in0=VV[:, t, :], in1=GVB, op=ALU.mult),
                     reads=[tVV[t], tGVB], writes=[tVV[t]])
            for n in range(4):
                wu = load_panel(d_wgi, 512 * n, 512)
                for oc in range(4):
                    g = 4 * n + oc
                    for (t0, nt) in tok_blocks(ntok):
                        bk = nb()
                        P.op("pe", [lambda e, k=k, bk=bk, wu=wu, oc=oc, t0=t0, nt=nt: e.matmul(
                            banks[bk][:, 0:nt], lhsT=WP[:, wu, k, 128 * oc:128 * oc + 128], rhs=HT[:, k, t0:t0 + nt],
                            start=(k == 0), stop=(k == 7)) for k in range(8)],
                            reads=[tWP[wu]] + ht_tiles(t0, nt), writes=[tB[bk]])
                        P.op("act", lambda e, g=g, bk=bk, t0=t0, nt=nt: e.activation(
                            out=UT[:, g, t0:t0 + nt], in_=banks[bk][:, 0:nt], func=AF.Gelu_apprx_tanh),
                            reads=[tB[bk]], writes=[tUT[i] for i in range(t0 // 128, (t0 + nt) // 128)])
            for t in range(ntiles):
                for gq in range(4):
                    bk = nb()
                    fns = []
                    for gg in range(4):
                        g = 4 * gq + gg
                        fns.append(lambda e, g=g, gg=gg, t=t, bk=bk: e.matmul(
                            banks[bk][:, 128 * gg:128 * gg + 128], lhsT=VV[:, t, 128 * g:128 * g + 128],
                            rhs=WST[:, g, :], start=True, stop=True))
                    P.op("pe", fns, reads=[tVV[t], tWST], writes=[tB[bk]])
                    P.op("dve", lambda e, gq=gq, t=t, bk=bk: e.tensor_tensor(
                        out=TMP[:, (4 * t + gq) % 2, :].rearrange("p (g q) -> p g q", q=128),
                        in0=banks[bk][:].rearrange("p (g q) -> p g q", q=128),
                        in1=BSB[:, 4 * gq:4 * gq + 4, :], op=ALU.add),
                        reads=[tB[bk], tBSB], writes=[tTMP[(4 * t + gq) % 2]])
                    P.op("dve", lambda e, gq=gq, t=t: e.tensor_tensor(
                        out=UT[:, 4 * gq:4 * gq + 4, 128 * t:128 * t + 128],
                        in0=UT[:, 4 * gq:4 * gq + 4, 128 * t:128 * t + 128],
                        in1=TMP[:, (4 * t + gq) % 2, :].rearrange("p (g q) -> p g q", q=128), op=ALU.mult),
                        reads=[tTMP[(4 * t + gq) % 2], tUT[t]], writes=[tUT[t]])
            WOv = [WP[:, i, :, :].rearrange("p k n -> p (k n)").rearrange("p (g m) -> p g m", m=1024)
                   for i in range(4)]
            for i in range(4):
                src = d_wgo[512 * i:512 * i + 512, :].rearrange("(g p) n -> p g n", p=128)
                P.dma("pool", lambda e, i=i, src=src: e.dma_start(out=WOv[i], in_=src),
                      writes=[tWP[i]] + (tDB if i == 3 else []))
            for t in range(ntiles):
                for hh in range(2):
                    bk = nb()
                    P.op("pe", [lambda e, g=g, t=t, hh=hh, bk=bk: e.matmul(
                        banks[bk][:], lhsT=UT[:, g, 128 * t:128 * t + 128],
                        rhs=WOv[g // 4][:, g % 4, 512 * hh:512 * hh + 512], start=(g == 0), stop=(g == 15))
                        for g in range(16)], reads=[tUT[t]] + tWP, writes=[tB[bk]])
                    residual(bk, t, hh, 2)

        BSB = carve(81920, 8192).bitcast(F32).rearrange("p (g q) -> p g q", q=128)
        tBSB = Tile("BSB")

        def load_bsb():
            P.dma("sp", lambda e: e.dma_start(
                out=BSB, in_=bass.AP(d_bsp.tensor, 0, [[0, 128], [128, 16], [1, 128]])), writes=[tBSB])

        def final_norm(ntiles, dst):
            GF = carve(90112, 4096).bitcast(F32)
            tGF = Tile("GF")
            CV, tCV = last_cv[0]
            P.dma("sp", lambda e: e.dma_start(out=GF, in_=bass.AP(t_gfin, 0, [[0, 128], [1, D]])), writes=[tGF])
            for tb in range(0, ntiles, 4):
                te = min(ntiles, tb + 4)
                for t in range(tb, te):
                    P.op("act", lambda e, t=t: e.activation(out=XS[:, 0, :], in_=X[:, t, :], func=AF.Square,
                                                            accum_out=SS[:, t:t + 1]),
                         reads=[tX[t]], writes=[tXS[0], tJ, tSS])
                rstd_from_ss(te - tb, 1.0 / D, tSS, SS[:, tb:te], RS[:, tb:te], tRS)
                for t in range(tb, te):
                    ci = t % 6
                    P.op("dve", lambda e, t=t, ci=ci: e.scalar_tensor_tensor(
                        out=CV[:, ci, :], in0=X[:, t, :], scalar=RS[:, t:t + 1], in1=GF, op0=ALU.mult, op1=ALU.mult),
                        reads=[tX[t], tRS, tGF], writes=[tCV[ci]])
                    P.dma("sp", lambda e, t=t, ci=ci: e.dma_start(out=dst[128 * t:128 * t + 128, :], in_=CV[:, ci, :]),
                          reads=[tCV[ci]], writes=[Tile("out")])

        def qkv(ntq, ntk, QT, KT, VX, tQT, tKT, tVX, prompt):
            P.op("pool", lambda e: e.memset(VX[:, :, :, 64:65], 1.0), writes=tVX, guard=True)
            chk("q_memset")
            if prompt:
                KF = carve(72000, 4096).bitcast(F32).rearrange("p (s t) -> p s t", t=512)
                KS2 = carve(76096, 4096).bitcast(F32).rearrange("p (s t) -> p s t", t=512)
                VS = carve(80192, 8192).bitcast(F32).rearrange("p (s t) -> p s t", t=512)
                tKF = [Tile("KF0"), Tile("KF1")]
                tKS = [Tile("KS0"), Tile("KS1")]
                tVS = [Tile(f"VS{i}") for i in range(4)]
            kcnt = [0]
            vcnt = [0]
            k_pend = [None]
            wv = [3, 2] if prompt else [2, 2]
            v_loaded = set()

            def v_load(n):
                if n in v_loaded:
                    return
                v_loaded.add(n)
                srcv = d_wqkv[:, 2048 + 512 * n:2048 + 512 * n + 512].rearrange("(k p) n -> p k n", p=128)
                P.dma("pool", lambda e: e.dma_start(out=WP[:, wv[n], :, :], in_=srcv),
                      writes=[tWP[wv[n]]] + (tDB if wv[n] == 3 else []))

            v_load(0)
            if prompt:
                v_load(1)

            def qk_iter(which, wi, oc, c, t0, nt):
                DST, tD = (QT, tQT) if which == 0 else (KT, tKT)
                bk = nb()
                P.op("pe", [lambda e, k=k: e.matmul(
                    banks[bk][:, 0:nt], lhsT=WP[:, wi, k, 128 * oc:128 * oc + 128],
                    rhs=HT[:, k, t0:t0 + nt], start=(k == 0), stop=(k == 7)) for k in range(8)],
                    reads=[tWP[wi]] + ht_tiles(t0, nt), writes=[tB[bk]])
                sc = 0.125 if which == 0 else 1.0
                P.op("act", lambda e: e.activation(
                    out=DST[:, c, t0:t0 + nt], in_=banks[bk][:, 0:nt], func=AF.Copy, scale=sc),
                    reads=[tB[bk]], writes=[tD[c]])
                if prompt and which == 1:
                    s_ = kcnt[0] % 2
                    kcnt[0] += 1
                    P.op("dve", lambda e: e.tensor_copy(out=KF[:, s_, :], in_=banks[bk][:]),
                         reads=[tB[bk]], writes=[tKF[s_]])

                    def later():
                        b2 = nb()
                        P.op("pe", [lambda e, i=i: e.transpose(
                            out=banks[b2][:, 128 * i:128 * i + 128], in_=KF[:, s_, 128 * i:128 * i + 128],
                            identity=IDF[:]) for i in range(4)], reads=[tKF[s_], tID], writes=[tB[b2]])
                        P.op("dve", lambda e: e.tensor_copy(out=KS2[:, s_, :], in_=banks[b2][:]),
                             reads=[tB[b2]], writes=[tKS[s_]])
                        for i in range(4):
                            t = t0 // 128 + i
                            dst = d_sk[t // 2, 2 * c:2 * c + 2, 128 * (t % 2):128 * (t % 2) + 128, :].rearrange(
                                "h s d -> s h d")
                            P.dma("sp", lambda e, i=i, dst=dst: e.dma_start(
                                out=dst, in_=KS2[:, s_, 128 * i:128 * i + 128].rearrange("p (h d) -> p h d", d=64)),
                                reads=[tKS[s_]], writes=[Tile("out")])

                    prev_later = k_pend[0]
                    k_pend[0] = later
                    if prev_later is not None:
                        prev_later()

            def v_iter(t, n):
                v_load(n)
                wi = wv[n]
                bk = nb()
                P.op("pe", [lambda e, k=k: e.matmul(
                    banks[bk][:], lhsT=HT[:, k, 128 * t:128 * t + 128], rhs=WP[:, wi, k, :],
                    start=(k == 0), stop=(k == 7)) for k in range(8)],
                    reads=[tWP[wi]] + ht_tiles(128 * t, 128), writes=[tB[bk]])
                P.op("act", lambda e: e.activation(
                    out=VX[:, t, 8 * n:8 * n + 8, 0:64], in_=banks[bk][:].rearrange("p (h d) -> p h d", d=64),
                    func=AF.Copy), reads=[tB[bk]], writes=[tVX[t]])
                if prompt:
                    s_ = vcnt[0] % 4
                    vcnt[0] += 1
                    P.op("dve", lambda e: e.tensor_copy(out=VS[:, s_, :], in_=banks[bk][:]),
                         reads=[tB[bk]], writes=[tVS[s_]])
                    dst = d_sv[t // 2, 8 * n:8 * n + 8, 128 * (t % 2):128 * (t % 2) + 128, :].rearrange("h s d -> s h d")
                    P.dma("sp", lambda e: e.dma_start(
                        out=dst, in_=VS[:, s_, :].rearrange("p (h d) -> p h d", d=64)),
                        reads=[tVS[s_]], writes=[Tile("out")])

            v_list = ([(t, n) for t in range(ntk) for n in range(2)] if prompt
                      else [(t, n) for n in range(2) for t in range(ntk)])
            qk_list = []
            pcount = 0
            for which, ntok in ((1, 128 * ntk), (0, 128 * ntq)):
                for n in range(2):
                    for oc in range(4):
                        for (t0, nt) in tok_blocks(ntok):
                            qk_list.append((which, n, oc, t0, nt))
            vptr = 0
            cur = None
            wi = 0
            for i, (which, n, oc, t0, nt) in enumerate(qk_list):
                if (which, n) != cur:
                    cur = (which, n)
                    wi = pcount % 2
                    pcount += 1
                    srcp = d_wqkv[:, 1024 * which + 512 * n:1024 * which + 512 * n + 512].rearrange(
                        "(k p) n -> p k n", p=128)
                    P.dma("pool", lambda e, wi=wi, srcp=srcp: e.dma_start(out=WP[:, wi, :, :], in_=srcp),
                          writes=[tWP[wi]])
                qk_iter(which, wi, oc, 4 * n + oc, t0, nt)
                want = ((i + 1) * len(v_list)) // len(qk_list)
                while vptr < want:
                    v_iter(*v_list[vptr])
                    vptr += 1
            if k_pend[0] is not None:
                k_pend[0]()
                k_pend[0] = None
            wp_rr[0] = 0

        def o_transpose(O, tO, oi, tile):
            bk = nb()
            P.op("pe", [lambda e, c=c: e.transpose(out=banks[bk][:].bitcast(BF16)[:, 128 * c:128 * c + 128],
                                                   in_=O[:, oi, 128 * c:128 * c + 128], identity=IDB[:])
                        for c in range(8)], reads=[tO[oi], tID], writes=[tB[bk]])
            P.op("act", lambda e: e.activation(
                out=HT[:, :, 128 * tile:128 * tile + 128],
                in_=banks[bk][:].bitcast(BF16).rearrange("p (c t) -> p c t", t=128), func=AF.Copy),
                reads=[tB[bk]], writes=list(tHT[tile // 4]))

        def out_proj(ntiles):
            flush_deferred()
            wis = [load_panel(d_wo, 512 * n, 512) for n in range(2)]
            for t in range(ntiles):
                for n in range(2):
                    wi = wis[n]
                    bk = nb()
                    P.op("pe", [lambda e, k=k, t=t, bk=bk, wi=wi: e.matmul(
                        banks[bk][:], lhsT=HT[:, k, 128 * t:128 * t + 128], rhs=WP[:, wi, k, :],
                        start=(k == 0), stop=(k == 7)) for k in range(8)],
                        reads=[tWP[wi]] + ht_tiles(128 * t, 128), writes=[tB[bk]])
                    residual(bk, t, n, 0)

        def pv_finish(bk, O, oi, tO, h, RC, tRC):
            P.op("dve", lambda e: e.reciprocal(out=RC[:, 0:1], in_=banks[bk][:, 64:65]), reads=[tB[bk]], writes=[tRC])
            P.op("dve", lambda e: e.tensor_scalar(out=O[:, oi, 64 * h:64 * h + 64], in0=banks[bk][:, 0:64],
                                                  scalar1=RC[:, 0:1], scalar2=None, op0=ALU.mult),
                 reads=[tB[bk], tRC], writes=[tO[oi]])

        def prompt_group():
            cond = 0
            for t in range(8):
                P.dma("sp", lambda e, t=t: e.dma_start(out=X[:, t, :], in_=d_xp[128 * t:128 * t + 128, :]),
                      writes=[tX[t]])
            P.snapshot()
            norm_mod(0, 0, 8, cond)
            P.barrier(snapshot=False)
            chk("p_norm")
            QT = carve(0, 16384).rearrange("p (c t) -> p c t", t=1024)
            KT = carve(16384, 16384).rearrange("p (c t) -> p c t", t=1024)
            VX = carve(32768, 8 * 16 * 65 * 2).rearrange("p (t h d) -> p t h d", h=16, d=65)
            O = carve(49408, 4096).rearrange("p (s f) -> p s f", f=1024)
            PT = carve(53504, 6144).rearrange("p (s k) -> p s k", k=1024)
            RC = carve(59648, 64).bitcast(F32)
            tQT, tKT = [Tile(f"QT{c}") for c in range(8)], [Tile(f"KT{c}") for c in range(8)]
            tVX = [Tile(f"VX{t}") for t in range(8)]
            tO = [Tile("O0"), Tile("O1")]
            tPT = [Tile("PT0"), Tile("PT1"), Tile("PT2")]
            tRC = Tile("RC")
            qkv(8, 8, QT, KT, VX, tQT, tKT, tVX, True)
            chk("p_qkv")
            def p_stage1(u, pi):
                b, qc, hg = u
                tile = 2 * b + qc
                bks = [nb(), nb()]
                fns = []
                for hh in range(4):
                    h = 4 * hg + hh
                    c, po = h // 2, 64 * (h % 2)
                    for kc in range(2):
                        fns.append(lambda e, hh=hh, kc=kc, c=c, po=po: e.matmul(
                            banks[bks[hh % 2]][:, 256 * (hh // 2) + 128 * kc:256 * (hh // 2) + 128 * kc + 128],
                            lhsT=KT[po:po + 64, c, 128 * (2 * b + kc):128 * (2 * b + kc) + 128],
                            rhs=QT[po:po + 64, c, 128 * tile:128 * tile + 128], start=True, stop=True))
                P.op("pe", fns, reads=[tQT[2 * hg], tKT[2 * hg], tQT[2 * hg + 1], tKT[2 * hg + 1]],
                     writes=[tB[bks[0]], tB[bks[1]]])
                for i2 in range(2):
                    P.op("act", lambda e, i2=i2: e.activation(out=PT[:, pi, 512 * i2:512 * i2 + 512],
                                                              in_=banks[bks[i2]][:], func=AF.Exp),
                         reads=[tB[bks[i2]]], writes=[tPT[pi]])

            def p_stage2(u, pi):
                b, qc, hg = u
                tile = 2 * b + qc
                oi = tile % 2
                b2 = nb()
                fns = []
                for hh in range(4):
                    h = 4 * hg + hh
                    for kc in range(2):
                        fns.append(lambda e, hh=hh, kc=kc, h=h: e.matmul(
                            banks[b2][:, 65 * hh:65 * hh + 65],
                            lhsT=PT[:, pi, 512 * (hh % 2) + 256 * (hh // 2) + 128 * kc:
                                    512 * (hh % 2) + 256 * (hh // 2) + 128 * kc + 128], rhs=VX[:, 2 * b + kc, h, :],
                            start=(kc == 0), stop=(kc == 1)))
                P.op("pe", fns, reads=[tPT[pi], tVX[2 * b], tVX[2 * b + 1]], writes=[tB[b2]])
                pvv = banks[b2][:, 0:260].rearrange("p (h d) -> p h d", d=65)
                P.op("dve", lambda e: e.reciprocal(out=RC[:, 0:4], in_=pvv[:, :, 64]), reads=[tB[b2]], writes=[tRC])
                P.op("dve", lambda e: e.tensor_tensor(
                    out=O[:, oi, 256 * hg:256 * hg + 256].rearrange("p (h d) -> p h d", d=64), in0=pvv[:, :, 0:64],
                    in1=RC[:, 0:4].unsqueeze(2).to_broadcast([128, 4, 64]), op=ALU.mult),
                    reads=[tB[b2], tRC], writes=[tO[oi]])
                if hg == 3:
                    o_transpose(O, tO, oi, tile)

            units = [(b, qc, hg) for b in range(4) for qc in range(2) for hg in range(4)]

            g_first, g_items = gate_riders(cond)
            g_first()
            riders = g_items + [ada_rider(0, n) for n in (6, 7, 8, 9)] + \
                [ada_rider(1, n) for n in (0, 1, 2, 3, 6, 7, 8, 9)]
            r_step, r_state = start_riders(riders, nbuf=4)
            for i, u in enumerate(units):
                if i >= 2 and i % 8 not in (3, 7):
                    r_step()
                p_stage1(u, i % 3)
                if i >= 2:
                    p_stage2(units[i - 2], (i - 2) % 3)
            assert r_state["next"] == len(riders)
            am_item(0, 1)[1](None)
            am_item(1, 0)[1](None)
            am_item(1, 1)[1](None)
            wp_rr[0] = 0
            for i in (len(units) - 2, len(units) - 1):
                p_stage2(units[i], i % 3)
            chk("p_attn")
            out_proj(8)
            chk("p_oproj")
            blocks = [(0, 512, 0, 512, 256), (512, 1024, 512, 1024, 256)]
            ffn(0, cond, blocks, list(range(8)), 8)
            flush_deferred()
            chk("p_ffn0")
            gmlp(cond, 8)
            chk("p_gmlp")
            ffn(1, cond, blocks, list(range(8)), 8)
            chk("p_ffn1")
            final_norm(8, d_yp)
            chk("p_final")

        def sample_group():
            cond = 1
            for t in range(8):
                P.dma("sp", lambda e, t=t: e.dma_start(out=X[:, t, :], in_=d_xs[128 * t:128 * t + 128, :]),
                      writes=[tX[t]])
            P.snapshot()
            norm_mod(0, 0, 8, cond)
            P.barrier(snapshot=False)
            QT = carve(0, 12288).rearrange("p (c t) -> p c t", t=768)
            KT = carve(12288, 16384).rearrange("p (c t) -> p c t", t=1024)
            VX = carve(28672, 8 * 16 * 65 * 2).rearrange("p (t h d) -> p t h d", h=16, d=65)
            OA = carve(45312, 12288).rearrange("p (s f) -> p s f", f=1024)
            KCT = carve(57600, 8192).rearrange("p (c t) -> p c t", t=512)
            VCX = carve(65792, 4 * 16 * 65 * 2).rearrange("p (a h d) -> p a h d", h=16, d=65)
            MINT = carve(74112, 2560).rearrange("p (s e k) -> p s e k", s=2, k=64)
            MTOP = carve(76672, 2560).rearrange("p (s e k) -> p s e k", s=2, k=64)
            PT = carve(79232, 2 * 1152 * 2).rearrange("p (s k) -> p s k", k=1152)
            GR = carve(83840, 2 * 3072).bitcast(F32).rearrange("p (s e k) -> p s e k", s=2, k=64)
            G2 = carve(89984, 2 * 2 * 1280).rearrange("p (w s e k) -> p w s e k", s=2, w=2, k=64)
            RC = carve(95104, 16).bitcast(F32)
            RPT = TMP[:, 0, 0:320].rearrange("p (a x) -> p a x", x=160)
            tQT, tKT = [Tile(f"QTs{c}") for c in range(8)], [Tile(f"KTs{c}") for c in range(8)]
            tVX = [Tile(f"VXs{t}") for t in range(8)]
            tOA = [Tile(f"OA{m}") for m in range(6)]
            tKCT, tVCX, tMSK = Tile("KCT"), Tile("VCX"), Tile("MSK")
            tPT = [Tile("PTs0"), Tile("PTs1")]
            tGR = [Tile("GR0"), Tile("GR1")]
            tG2 = [Tile("G20"), Tile("G21")]
            tRC, tRPT, tRP = Tile("RCs"), tTMP[0], Tile("RP")
            chk("s_pre")
            qkv(6, 8, QT, KT, VX, tQT, tKT, tVX, False)
            P.op("pool", lambda e: e.memset(RPT[0:120, :, :], 0.0), writes=[tRPT])
            P.dma("sp", lambda e: e.dma_start(out=RPT[0:120, :, 48:79], in_=d_rpb.rearrange("(a p) j -> p a j", p=120)),
                  reads=[tRPT], writes=[tRPT])
            P.dma("sp", lambda e: e.dma_start(out=t_rp.ap().rearrange("(a p) x -> p a x", p=120), in_=RPT[0:120, :, :]),
                  reads=[tRPT], writes=[tRP])
            P.dma("pool", lambda e: e.dma_start(out=MINT, in_=d_mint.rearrange("p (s e k) -> p s e k", s=2, k=64)),
                  writes=[tMSK], guard=True)
            P.dma("pool", lambda e: e.dma_start(out=MTOP, in_=d_mtop.rearrange("p (s e k) -> p s e k", s=2, k=64)),
                  writes=[tMSK], guard=True)
            wk = 3
            KC = WP[:, wk, :, :].rearrange("p k n -> p (k n)").rearrange("p (a h d) -> p a h d", h=16, d=64)
            for a in range(4):
                P.dma("pool", lambda e, a=a: e.dma_start(
                    out=KC[:, a, :, :], in_=d_ck[:, 128 * a:128 * a + 128, :].rearrange("h p d -> p h d")),
                    writes=[tWP[wk]] + tDB)
                P.dma("pool", lambda e, a=a: e.dma_start(
                    out=VCX[:, a, :, 0:64], in_=d_cv[:, 128 * a:128 * a + 128, :].rearrange("h p d -> p h d")),
                    writes=[tVCX], guard=True)
            P.op("pool", lambda e: e.memset(VCX[:, :, :, 64:65], 1.0), writes=[tVCX], guard=True)
            for a in range(4):
                bk = nb()
                P.op("pe", [lambda e, hp=hp, a=a, bk=bk: e.transpose(
                    out=banks[bk][:].bitcast(BF16)[:, 128 * hp:128 * hp + 128],
                    in_=KC[:, a, 2 * hp:2 * hp + 2, :].rearrange("p h d -> p (h d)"), identity=IDB[:])
                    for hp in range(8)], reads=[tWP[wk], tID] + tDB, writes=[tB[bk]])
                P.op("dve", lambda e, a=a, bk=bk: e.tensor_copy(
                    out=KCT[:, :, 128 * a:128 * a + 128],
                    in_=banks[bk][:].bitcast(BF16).rearrange("p (c t) -> p c t", t=128)),
                    reads=[tB[bk]], writes=[tKCT])
            chk("s_qkv")
            def s_plan(m):
                if m < 2:
                    return [0, 1, 2, 3], 1, [2 * j - 2 * m + 2 for j in range(4)]
                return list(range(m - 2, m + 3)), 0, [2 * (j - m) + 4 for j in range(m - 2, m + 3)]

            def build_tables(h):
                po = 64 * (h % 2)
                gs = h % 2
                for s_ in range(2):
                    src = bass.AP(t_rp, (h * 15 + 3 - s_) * 160, [[1, 64], [160, 12], [1, 64]])
                    P.dma("sp", lambda e, s_=s_, src=src: e.dma_start(
                        out=GR[po:po + 64, s_, :, :], in_=src), reads=[tRP], writes=[tGR[gs]])
                P.op("dve", lambda e: e.tensor_tensor(out=G2[po:po + 64, 0, :, :, :], in0=GR[po:po + 64, :, 0:10, :],
                                                      in1=MINT[po:po + 64, :, :, :], op=ALU.add),
                     reads=[tGR[gs], tMSK], writes=[tG2[gs]])
                P.op("dve", lambda e: e.tensor_tensor(out=G2[po:po + 64, 1, :, :, :], in0=GR[po:po + 64, :, 2:12, :],
                                                      in1=MTOP[po:po + 64, :, :, :], op=ALU.add),
                     reads=[tGR[gs], tMSK], writes=[tG2[gs]])

            def s_stage1(u, pi):
                h, m = u
                c, po = h // 2, 64 * (h % 2)
                gs = h % 2
                if m == 0 and h == 0:
                    build_tables(0)
                if m == 2 and h + 1 < NH:
                    build_tables(h + 1)
                js, tw, i0 = s_plan(m)
                nw = len(js)
                nch = nw + 4
                bks = [nb() for _ in range((nch + 3) // 4)]
                fns = []
                for i in range(nch):
                    outp = banks[bks[i // 4]][:, 128 * (i % 4):128 * (i % 4) + 128]
                    if i < nw:
                        j = js[i]
                        fns.append(lambda e, outp=outp, j=j: e.matmul(
                            outp, lhsT=KT[po:po + 64, c, 128 * j:128 * j + 128],
                            rhs=QT[po:po + 64, c, 128 * m:128 * m + 128], start=True, stop=False))
                        for s_ in range(2):
                            fns.append(lambda e, outp=outp, ii=i0[i], s_=s_: e.matmul(
                                outp[:, 64 * s_:64 * s_ + 64],
                                lhsT=G2[po:po + 64, tw, s_, ii:ii + 2, :].rearrange("p e k -> p (e k)"),
                                rhs=JB[po:po + 64, po:po + 64],
                                start=False, stop=(s_ == 1)))
                    else:
                        a_ = i - nw
                        fns.append(lambda e, outp=outp, a_=a_: e.matmul(
                            outp, lhsT=KCT[po:po + 64, c, 128 * a_:128 * a_ + 128],
                            rhs=QT[po:po + 64, c, 128 * m:128 * m + 128], start=True, stop=True))
                P.op("pe", fns, reads=[tQT[c], tKT[c], tKCT, tG2[gs], tID], writes=[tB[b_] for b_ in bks])
                for bi_, bk in enumerate(bks):
                    n_here = min(4, nch - 4 * bi_) * 128
                    P.op("act", lambda e, bk=bk, bi_=bi_, n_here=n_here: e.activation(
                        out=PT[:, pi, 512 * bi_:512 * bi_ + n_here], in_=banks[bk][:, 0:n_here], func=AF.Exp),
                        reads=[tB[bk]], writes=[tPT[pi]])

            def s_stage2(u, pi):
                h, m = u
                js, tw, i0 = s_plan(m)
                nw = len(js)
                nch = nw + 4
                b2 = nb()
                fns = []
                for i in range(nch):
                    rhs = VX[:, js[i], h, :] if i < nw else VCX[:, i - nw, h, :]
                    fns.append(lambda e, i=i, rhs=rhs: e.matmul(
                        banks[b2][:, 0:65], lhsT=PT[:, pi, 128 * i:128 * i + 128], rhs=rhs,
                        start=(i == 0), stop=(i == nch - 1)))
                P.op("pe", fns, reads=[tPT[pi], tVCX] + [tVX[j] for j in js], writes=[tB[b2]])
                pv_finish(b2, OA, m, tOA, h, RC, tRC)

            units = [(h, m) for h in range(NH) for m in range(6)]
            g_first, g_items = gate_riders(cond)
            g_first()
            r_step, r_state = start_riders(g_items)
            for i, u in enumerate(units):
                if i % 8 == 4:
                    r_step()
                s_stage1(u, i % 2)
                if i:
                    s_stage2(units[i - 1], (i - 1) % 2)
            s_stage2(units[-1], (len(units) - 1) % 2)
            assert r_state["next"] == 8
            wp_rr[0] = 0
            chk("s_attn")
            for m in range(6):
                o_transpose(OA, tOA, m, m)
            out_proj(6)
            chk("s_oproj")
            ffn(0, cond, [(0, 321, 0, 320, 0), (319, 641, 320, 640, 0)], list(range(5)), 6)
            gmlp(cond, 5)
            ffn(1, cond, [(0, 257, 0, 256, 0), (255, 513, 256, 512, 0)], list(range(4)), 5)
            final_norm(4, d_ys)

        try:
            chk("setup")
            if do_prompt:
                prompt_group()
            if do_sample:
                sample_group()
        except _Stop:
            pass
        P.finish()
        with nc.Block() as block:
            P.emit(block)
    return nc


def _masks(parity):
    qc = np.arange(64)
    kc = np.arange(64)
    cs = np.clip(qc - 8, 0, 48)
    validA = (kc[None, :] >= cs[:, None]) & (kc[None, :] < cs[:, None] + 16)
    if parity:
        validA = validA[::-1, ::-1]
    colmask = np.where(validA, 0.0, NEG).astype(np.float32)
    lo, hi = (-4, 3) if parity == 0 else (-3, 4)
    mint = np.zeros((2, 64, 2, 10, 64), np.float32)
    mtop = np.zeros((2, 64, 2, 10, 64), np.float32)
    for half in range(2):
        for pp in range(64):
            q = 63 - pp
            for s in range(2):
                for ei in range(10):
                    dl = ei - 4 - s
                    mint[half, pp, s, ei] = colmask[q] if lo <= dl <= hi else NEG
                    mtop[half, pp, s, ei] = colmask[q]
    return mint.reshape(128, 1280), mtop.reshape(128, 1280)


def _core_inputs(core, I):
    par = core % 2
    b = core // 2
    rv = (lambda a, ax: np.flip(a, axis=ax)) if par else (lambda a, ax: a)
    xp = rv(I["x_prompt"][4 * core:4 * core + 4], 1).reshape(1024, D)
    xs = rv(I["x_sample"][b], 0)
    rows = np.zeros((N_ROWS, 128), np.float32)
    rows[R_BADA:R_BADA + 96] = I["b_ada"].reshape(96, 128)
    rows[R_CCTX:R_CCTX + 8] = I["c_ctx"].reshape(8, 128)
    rows[R_C:R_C + 8] = I["c"][b].reshape(8, 128)
    rows[R_GMIX:R_GMIX + 16] = I["norm_mix_g"].reshape(16, 128)
    rows[R_GFFN:R_GFFN + 16] = I["norm_ffn_g"].reshape(16, 128)
    rows[R_GFIN:R_GFIN + 8] = I["norm_final_g"].reshape(8, 128)
    wc = rv(I["w_ffn_conv"], 1)
    rows[R_WC:R_WC + 264] = wc.reshape(264, 128)
    rows[R_BC:R_BC + 88] = I["b_ffn_conv"].reshape(88, 128)
    jm = np.zeros((128, 128), np.float32)
    for s in range(2):
        for p in range(64):
            jm[64 * s + p, 64 * s + 63 - p] = 1.0
    mint, mtop = _masks(par)
    m = {
        "xp": xp, "xs": xs,
        "ck": I["cache_k"][b, 0], "cv": I["cache_v"][b, 0],
        "rows": rows, "ident": np.eye(128, dtype=np.float32), "jmat": jm,
        "mint": mint, "mtop": mtop,
        "w_ada": I["w_ada"], "b_ada": I["b_ada"],
        "w_qkv": I["w_qkv"][0], "w_o": I["w_attn_out"][0],
        "rpb": rv(rv(I["rpb"][0], 1), 2).reshape(NH * 15, 31),
        "w_gi": I["w_gmlp_in"][0], "g_v": I["g_gmlp_v"][0],
        "w_sp": rv(rv(I["w_spatial"][0], 1), 2), "b_sp": rv(I["b_spatial"][0], 1),
        "w_go": I["w_gmlp_out"][0],
        "w_up": I["w_ffn_up"], "w_dn": I["w_ffn_down"], "g_fin": I["norm_final_g"],
    }
    return {k: np.ascontiguousarray(v, dtype=np.float32) for k, v in m.items()}


_NC_CACHE = {}


def kernel(**inputs):
    I = {k: np.asarray(v) for k, v in inputs.items()}
    if "nc" not in _NC_CACHE:
        _NC_CACHE["nc"] = build_program()
    nc = _NC_CACHE["nc"]
    in_maps = [_core_inputs(c, I) for c in range(8)]
    res = run_bass_kernel_spmd(nc, in_maps, core_ids=list(range(8)))
    y_prompt = np.zeros((32, 256, D), np.float32)
    y_sample = np.zeros((4, 1024, D), np.float32)
    state_k = np.zeros((32, 1, NH, 256, 64), np.float32)
    state_v = np.zeros((32, 1, NH, 256, 64), np.float32)
    for c in range(8):
        r = res.results[c]
        par, b = c % 2, c // 2
        yp = np.asarray(r["yp"]).reshape(4, 256, D)
        ys = np.asarray(r["ys"])
        sk = np.asarray(r["sk"])
        sv = np.asarray(r["sv"])
        if par:
            yp = yp[:, ::-1]
            ys = ys[::-1]
            sk = sk[:, :, ::-1]
            sv = sv[:, :, ::-1]
            y_sample[b, 512:] = ys
        else:
            y_sample[b, :512] = ys
        y_prompt[4 * c:4 * c + 4] = yp
        state_k[4 * c:4 * c + 4, 0] = sk
        state_v[4 * c:4 * c + 4, 0] = sv
    return (y_prompt, y_sample, state_k, state_v)
```

```python
import contextlib
import numpy as np
import concourse.bass as bass
import concourse.mybir as mybir
from concourse.bass_utils import run_bass_kernel_spmd

F32 = mybir.dt.float32
BF16 = mybir.dt.bfloat16
AF = mybir.ActivationFunctionType
ALU = mybir.AluOpType

D = 1024
NH = 16
FF = 2816
EE = 2048
EPS = 1e-6
NEG = -30000.0
ENGS = ("pe", "act", "dve", "pool", "sp")

R_BADA = 0
R_CCTX = 96
R_C = 104
R_GMIX = 112
R_GFFN = 128
R_GFIN = 144
R_WC = 152
R_BC = 416
N_ROWS = 512


class Tile:
    __slots__ = ("name", "w", "r", "excl", "strict")

    def __init__(self, name, excl=False, strict=False):
        self.name = name
        self.w = {}
        self.r = {}
        self.excl = excl
        self.strict = strict


class Prog:
    def __init__(self, nc, stack, n_dma_sems=16):
        self.nc = nc
        self.sems = {e: stack.enter_context(nc.semaphore("sem_" + e)) for e in ENGS}
        self.cnt = {e: 0 for e in ENGS}
        self.lists = {e: [] for e in ENGS}
        self.waited = {e: {} for e in ENGS}
        self.dma_sems = {}
        self.dma_rr = {}
        self.dma_val = {}
        self.guard = None
        for q in ("sp", "pool"):
            self.dma_sems[q] = [stack.enter_context(nc.semaphore(f"dsem_{q}_{i}"))
                                for i in range(n_dma_sems)]
            self.dma_rr[q] = 0

    def _sem(self, key):
        if isinstance(key, str):
            return self.sems[key]
        return self.dma_sems[key[0]][key[1]]

    def _wait(self, e, key, val):
        if self.waited[e].get(key, 0) >= val:
            return
        self.waited[e][key] = val
        self.lists[e].append(("wait", key, val))

    def _deps(self, e, reads, writes, is_dma=False):
        for t in reads:
            for key, (val, src) in t.w.items():
                if src == e and e == "pe" and not is_dma:
                    continue
                self._wait(e, key, val)
            if t.excl:
                for key, (val, src) in t.r.items():
                    if src != e:
                        self._wait(e, key, val)
        for t in writes:
            for key, (val, src) in t.w.items():
                if src == e and e == "pe" and not is_dma:
                    continue
                self._wait(e, key, val)
            for key, (val, src) in t.r.items():
                if src == e and e == "pe" and not is_dma:
                    continue
                self._wait(e, key, val)

    def snapshot(self):
        g = Tile("guard")
        for f in ENGS:
            if self.cnt[f]:
                g.w[f] = (self.cnt[f], f)
        for key, val in self.dma_val.items():
            g.w[key] = (val, "dma")
        self.guard = g

    def barrier(self, snapshot=True):
        if snapshot:
            self.snapshot()
        for e in ENGS:
            if e in ("pool", "pe"):
                continue
            for f in ENGS:
                if f != e and self.cnt[f]:
                    self._wait(e, f, self.cnt[f])
            for key, val in self.dma_val.items():
                self._wait(e, key, val)

    @staticmethod
    def _mark(key, val, src, reads, writes):
        for t in reads:
            t.r[key] = (val, src)
        for t in writes:
            t.w[key] = (val, src)

    def op(self, e, fns, reads=(), writes=(), guard=False):
        if callable(fns):
            fns = [fns]
        if guard and self.guard is not None:
            self._deps(e, (), [self.guard], is_dma=True)
        self._deps(e, reads, writes)
        self.cnt[e] += 1
        self.lists[e].append(("op", fns))
        self._mark(e, self.cnt[e], e, reads, writes)

    def dma(self, q, fn, reads=(), writes=(), guard=False):
        if guard and self.guard is not None:
            self._deps(q, (), [self.guard], is_dma=True)
        self._deps(q, reads, writes, is_dma=True)
        i = self.dma_rr[q]
        self.dma_rr[q] = (i + 1) % len(self.dma_sems[q])
        key = (q, i)
        prev = self.dma_val.get(key, 0)
        if prev:
            self._wait(q, key, prev)
        val = prev + 16
        self.dma_val[key] = val
        self.lists[q].append(("dma", fn, key))
        self._mark(key, val, "dma", reads, writes)

    def finish(self):
        for key, val in self.dma_val.items():
            self._wait("sp", key, val)

    def emit(self, block):
        prog = self

        def run(e, eng):
            for item in prog.lists[e]:
                if item[0] == "wait":
                    eng.wait_ge(prog._sem(item[1]), item[2])
                elif item[0] == "op":
                    ins = None
                    for f in item[1]:
                        ins = f(eng)
                    ins.then_inc(prog.sems[e], 1)
                else:
                    item[1](eng).then_inc(prog._sem(item[2]), 16)

        @block.sync
        def _(eng):
            run("sp", eng)

        @block.scalar
        def _(eng):
            run("act", eng)

        @block.vector
        def _(eng):
            run("dve", eng)

        @block.gpsimd
        def _(eng):
            run("pool", eng)

        @block.tensor
        def _(eng):
            run("pe", eng)


class _Stop(Exception):
    pass


def build_program(do_prompt=True, do_sample=True, stop=None):
    nc = bass.Bass("TRN2", target_bir_lowering=False)

    def chk(name):
        if stop == name:
            raise _Stop()

    def din(name, shape):
        return nc.dram_tensor(name, list(shape), F32, kind="ExternalInput")

    def dout(name, shape):
        return nc.dram_tensor(name, list(shape), F32, kind="ExternalOutput")

    d_xp = din("xp", [1024, D]).ap()
    d_xs = din("xs", [1024, D]).ap()
    d_ck = din("ck", [NH, 512, 64]).ap()
    d_cv = din("cv", [NH, 512, 64]).ap()
    d_rows = din("rows", [N_ROWS, 128]).ap()
    d_ident = din("ident", [128, 128]).ap()
    d_jmat = din("jmat", [128, 128]).ap()
    d_mint = din("mint", [128, 1280]).ap()
    d_mtop = din("mtop", [128, 1280]).ap()
    t_wada = din("w_ada", [2, D, 6 * D])
    t_bada = din("b_ada", [2, 6 * D])
    d_wqkv = din("w_qkv", [D, 3 * D]).ap()
    d_wo = din("w_o", [D, D]).ap()
    d_rpb = din("rpb", [NH * 15, 31]).ap()
    d_wgi = din("w_gi", [D, 2 * EE]).ap()
    t_gv = din("g_v", [EE])
    d_wsp = din("w_sp", [NH, 128, 128]).ap()
    d_bsp = din("b_sp", [NH, 128]).ap()
    d_wgo = din("w_go", [EE, D]).ap()
    t_wup = din("w_up", [2, D, 2 * FF])
    t_wdn = din("w_dn", [2, FF, D])
    t_gfin = din("g_fin", [D])
    t_rp = nc.dram_tensor("rp_scratch", [NH * 15, 160], F32)
    d_yp = dout("yp", [1024, D]).ap()
    d_ys = dout("ys", [512, D]).ap()
    d_sk = dout("sk", [4, NH, 256, 64]).ap()
    d_sv = dout("sv", [4, NH, 256, 64]).ap()
    d_wada = t_wada.ap()
    d_wup = t_wup.ap()
    d_wdn = t_wdn.ap()

    with contextlib.ExitStack() as st:
        P = Prog(nc, st)

        def sb(name, shape, dt):
            return st.enter_context(nc.sbuf_tensor(name, list(shape), dt))

        X = sb("X", [128, 8, D], F32)
        tX = [Tile(f"X{t}") for t in range(8)]
        HT = sb("HT", [128, 8, 1024], BF16)
        tHT = [[Tile(f"HT{b}_{c}") for c in range(8)] for b in range(2)]
        WP = sb("WP", [128, 4, 8, 512], BF16)
        tWP = [Tile(f"WP{i}") for i in range(4)]
        GT = sb("GT", [128, 4, D], F32)
        tGT = [Tile(f"GT{i}") for i in range(4)]
        CT = sb("CT", [128, N_ROWS], F32)
        tCT = Tile("CT")
        MOD = sb("MOD", [128, 2, 48, 2], F32)
        tMOD = Tile("MOD")
        AM = sb("AM", [128, 2, 2, 8, 2], F32)
        tAM = Tile("AM")
        IDF = sb("IDF", [128, 128], F32)
        IDB = sb("IDB", [128, 128], BF16)
        JB = sb("JB", [128, 128], BF16)
        tID = Tile("ID")
        SC = sb("SC", [128, 16], BF16)
        tSC = Tile("SC")
        SS = sb("SS", [128, 8], F32)
        RS = sb("RS", [128, 8], F32)
        tSS = Tile("SS")
        tRS = Tile("RS")
        XS = sb("XS", [128, 2, D], BF16)
        tXS = [Tile("XS0"), Tile("XS1")]
        TMP = sb("TMP", [128, 2, 512], F32)
        tTMP = [Tile("TMP0"), Tile("TMP1")]
        ARENA_B = 95232
        AR = sb("AR", [128, ARENA_B // 2], BF16)

        def carve(off, nbytes):
            assert off % 4 == 0 and off + nbytes <= ARENA_B, (off, nbytes)
            return AR[:, off // 2:(off + nbytes) // 2]

        banks = [st.enter_context(nc.psum_tensor(f"bank{i}", [128, 512], F32)) for i in range(8)]
        tB = [Tile(f"bank{i}", excl=True) for i in range(8)]
        bank_rr = [0]

        def nb():
            i = bank_rr[0]
            bank_rr[0] = (i + 1) % 8
            return i

        wp_rr = [0]

        def next_wp():
            i = wp_rr[0]
            wp_rr[0] = (i + 1) % 3
            return i

        def load_panel(src2d, col0, ncols, row0=0, nrows=1024, wi=None):
            if wi is None:
                wi = next_wp()
            nk = nrows // 128
            src = src2d[row0:row0 + nrows, col0:col0 + ncols].rearrange("(k p) n -> p k n", p=128)
            P.dma("pool", lambda e: e.dma_start(out=WP[:, wi, 0:nk, 0:ncols], in_=src),
                  writes=[tWP[wi]])
            return wi

        _t3 = [Tile("IDF"), Tile("IDBt"), Tile("JBt")]
        P.dma("sp", lambda e: e.dma_start(out=IDF[:], in_=d_ident), writes=[_t3[0]])
        P.dma("pool", lambda e: e.dma_start(out=IDB[:], in_=d_ident), writes=[_t3[1]])
        P.dma("pool", lambda e: e.dma_start(out=JB[:], in_=d_jmat), writes=[_t3[2]])
        for _t in _t3:
            tID.w.update(_t.w)
        RW = carve(0, 4 * 128 * 4).bitcast(F32).rearrange("p (b c) -> p b c", c=128)
        tRW = Tile("RW")
        P.dma("sp", lambda e: e.dma_start(out=RW, in_=d_rows.rearrange("(b p) c -> p b c", p=128)),
              writes=[tRW])
        for n in range(4):
            src0 = d_wada[0][:, 512 * n:512 * n + 512].rearrange("(k p) n -> p k n", p=128)
            P.dma("pool", lambda e, n=n, src0=src0: e.dma_start(out=WP[:, n, :, :], in_=src0),
                  writes=[tWP[n]])
        bi = nb()
        P.op("pe", [lambda e, b=b: e.transpose(out=banks[bi][:, 128 * b:128 * b + 128],
                                               in_=RW[:, b, :], identity=IDF[:])
                    for b in range(4)], reads=[tRW, tID], writes=[tB[bi]])
        P.op("dve", lambda e: e.tensor_copy(out=CT[:], in_=banks[bi][:]), reads=[tB[bi]], writes=[tCT])
        P.op("act", lambda e: e.activation(out=SC[:], in_=CT[:, R_CCTX:R_CCTX + 16], func=AF.Silu),
             reads=[tCT], writes=[tSC])
        SCv = SC[:].rearrange("p (c k) -> p k c", c=2)
        deferred = []
        pending = [None]
        db_rr = [0]
        tDB = [Tile("DB0"), Tile("DB1")]

        def db_load(src2d, col0):
            i = db_rr[0]
            db_rr[0] = 1 - i
            src = src2d[:, col0:col0 + 256].rearrange("(k p) n -> p k n", p=128)
            P.dma("pool", lambda e: e.dma_start(out=WP[:, 3, :, 256 * i:256 * i + 256], in_=src),
                  writes=[tDB[i], tWP[3]])
            return i

        def run_deferred(n=1):
            for _ in range(n):
                if pending[0] is None and deferred:
                    it = deferred.pop(0)
                    pending[0] = (it, it[0]())
                if pending[0] is None:
                    return
                cur, cur_buf = pending[0]
                if deferred:
                    nxt = deferred.pop(0)
                    pending[0] = (nxt, nxt[0]())
                else:
                    pending[0] = None
                cur[1](cur_buf)

        def flush_deferred():
            while deferred or pending[0] is not None:
                run_deferred(1)

        def ada_item(l, n2):
            def load():
                return db_load(d_wada[l], 256 * n2)

            def compute(i):
                bm = nb()
                fns = []
                for oc in range(2):
                    for k in range(8):
                        fns.append(lambda e, oc=oc, k=k: e.matmul(
                            banks[bm][:, 2 * oc:2 * oc + 2], lhsT=WP[:, 3, k, 256 * i + 128 * oc:256 * i + 128 * oc + 128],
                            rhs=SCv[:, k, :], start=(k == 0), stop=(k == 7)))
                P.op("pe", fns, reads=[tDB[i], tSC], writes=[tB[bm]])
                r0 = R_BADA + 48 * l + 2 * n2
                for q_ in range(2):
                    P.op("act", lambda e, q_=q_: e.activation(
                        out=MOD[:, l, 2 * n2 + q_, :], in_=banks[bm][:, 2 * q_:2 * q_ + 2], func=AF.Identity,
                        bias=CT[:, r0 + q_:r0 + q_ + 1]), reads=[tB[bm], tCT], writes=[tMOD])
            return (load, compute)

        def am_item(l, wh):
            def compute(_):
                j = 1 if wh == 0 else 4
                g0 = (R_GMIX if wh == 0 else R_GFFN) + 8 * l
                P.op("dve", lambda e: e.tensor_scalar(
                    out=AM[:, l, wh, :, :], in0=MOD[:, l, 8 * j:8 * j + 8, :], scalar1=1.0, scalar2=None,
                    op0=ALU.add), reads=[tMOD], writes=[tAM])
                P.op("dve", lambda e: e.tensor_tensor(
                    out=AM[:, l, wh, :, :], in0=AM[:, l, wh, :, :],
                    in1=CT[:, g0:g0 + 8].unsqueeze(2).to_broadcast([128, 8, 2]), op=ALU.mult),
                    reads=[tAM, tCT], writes=[tAM])
            return (lambda: None, compute)

        for n in range(4):
            bm0 = nb()
            P.op("pe", [lambda e, n=n, oc=oc, k=k, bm0=bm0: e.matmul(
                banks[bm0][:, 2 * oc:2 * oc + 2], lhsT=WP[:, n, k, 128 * oc:128 * oc + 128],
                rhs=SCv[:, k, :], start=(k == 0), stop=(k == 7)) for oc in range(4) for k in range(8)],
                reads=[tWP[n], tSC], writes=[tB[bm0]])
            for q_ in range(4):
                P.op("act", lambda e, n=n, q_=q_, bm0=bm0: e.activation(
                    out=MOD[:, 0, 4 * n + q_, :], in_=banks[bm0][:, 2 * q_:2 * q_ + 2], func=AF.Identity,
                    bias=CT[:, R_BADA + 4 * n + q_:R_BADA + 4 * n + q_ + 1]), reads=[tB[bm0], tCT], writes=[tMOD])
        am_item(0, 0)[1](None)
        def defer_layer1():
            for n2 in range(24):
                deferred.append(ada_item(1, n2))
            deferred.append(am_item(1, 0))
            deferred.append(am_item(1, 1))

        SCB = XS[:, 1, :].rearrange("p (k c) -> p k c", c=128)

        def defer_gates(cond):
            def first(_):
                P.op("dve", lambda e: e.tensor_copy(
                    out=SCB, in_=SC[:, 8 * cond:8 * cond + 8].unsqueeze(2).to_broadcast([128, 8, 128])),
                    reads=[tSC], writes=[tXS[1]])
                for gi in range(4):
                    l, wh = gi // 2, gi % 2
                    c0 = 2048 if wh == 0 else 5120
                    src = bass.AP(t_bada, l * 6 * D + c0, [[0, 128], [1, D]])
                    P.dma("sp", lambda e, gi=gi, src=src: e.dma_start(out=GT[:, gi, :], in_=src),
                          writes=[tGT[gi]])

            def gate_item(gi, qq):
                l, wh = gi // 2, gi % 2
                c0 = (2048 if wh == 0 else 5120) + 256 * qq

                def load():
                    return db_load(d_wada[l], c0)

                def compute(i):
                    bk = nb()
                    P.op("pe", [lambda e, k=k: e.matmul(
                        banks[bk][:, 0:256], lhsT=SCB[:, k, :], rhs=WP[:, 3, k, 256 * i:256 * i + 256],
                        start=(k == 0), stop=(k == 7)) for k in range(8)],
                        reads=[tXS[1], tDB[i]], writes=[tB[bk]])
                    P.op("dve", lambda e: e.tensor_tensor(
                        out=GT[:, gi, 256 * qq:256 * qq + 256], in0=banks[bk][:, 0:256],
                        in1=GT[:, gi, 256 * qq:256 * qq + 256], op=ALU.add),
                        reads=[tB[bk], tGT[gi]], writes=[tGT[gi]])
                return (load, compute)

            deferred.append((lambda: None, first))
            for gi in range(4):
                for qq in range(4):
                    deferred.append(gate_item(gi, qq))

        def gate_riders(cond):
            def first():
                P.op("dve", lambda e: e.tensor_copy(
                    out=SCB, in_=SC[:, 8 * cond:8 * cond + 8].unsqueeze(2).to_broadcast([128, 8, 128])),
                    reads=[tSC], writes=[tXS[1]])
                for gi in range(4):
                    l, wh = gi // 2, gi % 2
                    c0 = 2048 if wh == 0 else 5120
                    src = bass.AP(t_bada, l * 6 * D + c0, [[0, 128], [1, D]])
                    P.dma("sp", lambda e, gi=gi, src=src: e.dma_start(out=GT[:, gi, :], in_=src),
                          writes=[tGT[gi]])

            def item(j):
                gi, hh = j // 2, j % 2
                l, wh = gi // 2, gi % 2
                c0 = (2048 if wh == 0 else 5120) + 512 * hh

                def compute(buf):
                    bk = nb()
                    P.op("pe", [lambda e, k=k: e.matmul(
                        banks[bk][:], lhsT=SCB[:, k, :], rhs=WP[:, buf, k, :],
                        start=(k == 0), stop=(k == 7)) for k in range(8)],
                        reads=[tXS[1], tWP[buf]], writes=[tB[bk]])
                    P.op("dve", lambda e: e.tensor_tensor(
                        out=GT[:, gi, 512 * hh:512 * hh + 512], in0=banks[bk][:],
                        in1=GT[:, gi, 512 * hh:512 * hh + 512], op=ALU.add),
                        reads=[tB[bk], tGT[gi]], writes=[tGT[gi]])
                return (d_wada[l], c0, compute)
            return first, [item(j) for j in range(8)]

        def ada_rider(l, n):
            def compute(buf):
                bmj = nb()
                P.op("pe", [lambda e, oc=oc, k=k: e.matmul(
                    banks[bmj][:, 2 * oc:2 * oc + 2], lhsT=WP[:, buf, k, 128 * oc:128 * oc + 128],
                    rhs=SCv[:, k, :], start=(k == 0), stop=(k == 7)) for oc in range(4) for k in range(8)],
                    reads=[tWP[buf], tSC], writes=[tB[bmj]])
                for q_ in range(4):
                    r_ = R_BADA + 48 * l + 4 * n + q_
                    P.op("act", lambda e, q_=q_, r_=r_: e.activation(
                        out=MOD[:, l, 4 * n + q_, :], in_=banks[bmj][:, 2 * q_:2 * q_ + 2], func=AF.Identity,
                        bias=CT[:, r_:r_ + 1]), reads=[tB[bmj], tCT], writes=[tMOD])
            return (d_wada[l], 512 * n, compute)

        def start_riders(riders, nbuf=3):
            st_ = {"next": 0}

            def load(j):
                if j < len(riders):
                    src2d, col0, _ = riders[j]
                    srcj = src2d[:, col0:col0 + 512].rearrange("(k p) n -> p k n", p=128)
                    bi_ = j % nbuf
                    P.dma("pool", lambda e: e.dma_start(out=WP[:, bi_, :, :], in_=srcj),
                          writes=[tWP[bi_]] + (tDB if bi_ == 3 else []))

            def step():
                j = st_["next"]
                if j < len(riders):
                    load(j + nbuf - 1)
                    riders[j][2](j % nbuf)
                    st_["next"] = j + 1

            for j0 in range(nbuf - 1):
                load(j0)
            return step, st_

        def rstd_from_ss(n, scale, tin, SSap, RSap, tout):
            P.op("dve", lambda e: e.tensor_scalar(out=RSap, in0=SSap, scalar1=scale, scalar2=EPS,
                                                  op0=ALU.mult, op1=ALU.add), reads=[tin], writes=[tout])
            P.op("act", lambda e: e.activation(out=RSap, in_=RSap, func=AF.Sqrt), reads=[tout], writes=[tout])
            P.op("dve", lambda e: e.reciprocal(out=RSap, in_=RSap), reads=[tout], writes=[tout])

        tJ = Tile("junk", strict=True)

        def sumsq(ntiles):
            for tb in range(0, ntiles, 4):
                te = min(ntiles, tb + 4)
                for t in range(tb, te):
                    P.op("act", lambda e, t=t: e.activation(out=XS[:, 0, :], in_=X[:, t, :], func=AF.Square,
                                                            accum_out=SS[:, t:t + 1]),
                         reads=[tX[t]], writes=[tXS[0], tJ, tSS])
                rstd_from_ss(te - tb, 1.0 / D, tSS, SS[:, tb:te], RS[:, tb:te], tRS)

        def norm_mod(l, wh, ntiles, cond):
            jsh = 0 if wh == 0 else 3
            blks = [(tb, min(ntiles, tb + 4)) for tb in range(0, ntiles, 4)]

            def squares(tb, te):
                for t in range(tb, te):
                    P.op("act", lambda e, t=t: e.activation(out=XS[:, 0, :], in_=X[:, t, :], func=AF.Square,
                                                            accum_out=SS[:, t:t + 1]),
                         reads=[tX[t]], writes=[tXS[0], tJ, tSS])

            def scale_T(tb, te):
                bks = [nb() for _ in range(4)]
                for t in range(tb, te):
                    xi = 1 if wh_xs[0] else t % 2
                    P.op("dve", lambda e, t=t, xi=xi: e.tensor_scalar(
                        out=XS[:, xi, :], in0=X[:, t, :], scalar1=RS[:, t:t + 1], scalar2=None,
                        op0=ALU.mult), reads=[tX[t], tRS], writes=[tXS[xi]])
                    P.op("pe", [lambda e, c=c, t=t, xi=xi: e.transpose(
                        out=banks[bks[c // 2]][:].bitcast(BF16)[:, (c % 2) * 512 + (t - tb) * 128:
                                                                 (c % 2) * 512 + (t - tb) * 128 + 128],
                        in_=XS[:, xi, 128 * c:128 * c + 128], identity=IDB[:]) for c in range(8)],
                        reads=[tXS[xi], tID], writes=[tB[b_] for b_ in bks])
                return bks

            def evac(tb, te, bks):
                ntb = te - tb
                for c in range(8):
                    P.op("act", lambda e, c=c: e.activation(
                        out=HT[:, c, 128 * tb:128 * te],
                        in_=banks[bks[c // 2]][:].bitcast(BF16)[:, (c % 2) * 512:(c % 2) * 512 + 128 * ntb],
                        func=AF.Identity, scale=AM[:, l, wh, c, cond:cond + 1],
                        bias=MOD[:, l, 8 * jsh + c, cond:cond + 1]),
                        reads=[tB[bks[c // 2]], tAM, tMOD], writes=[tHT[tb // 4][c]])

            wh_xs = [len(blks) > 1]
            squares(*blks[0])
            rstd_from_ss(0, 1.0 / D, tSS, SS[:, blks[0][0]:blks[0][1]], RS[:, blks[0][0]:blks[0][1]], tRS)
            pend = None
            for bi_, (tb, te) in enumerate(blks):
                if bi_ + 1 < len(blks):
                    squares(*blks[bi_ + 1])
                else:
                    wh_xs[0] = False
                bks = scale_T(tb, te)
                if bi_ + 1 < len(blks):
                    nb0, nb1 = blks[bi_ + 1]
                    rstd_from_ss(0, 1.0 / D, tSS, SS[:, nb0:nb1], RS[:, nb0:nb1], tRS)
                if pend is not None:
                    evac(*pend)
                pend = (tb, te, bks)
            evac(*pend)

        def residual(bk, t, hh, gi, eng="dve"):
            ti = (2 * t + hh) % 2
            P.op("dve", lambda e: e.tensor_tensor(out=TMP[:, ti, :], in0=banks[bk][:],
                                                  in1=GT[:, gi, 512 * hh:512 * hh + 512], op=ALU.mult),
                 reads=[tB[bk], tGT[gi]], writes=[tTMP[ti]])
            P.op(eng, lambda e: e.tensor_tensor(out=X[:, t, 512 * hh:512 * hh + 512],
                                                   in0=X[:, t, 512 * hh:512 * hh + 512],
                                                   in1=TMP[:, ti, :], op=ALU.add),
                 reads=[tTMP[ti], tX[t]], writes=[tX[t]])

        def tok_blocks(ntok):
            out = []
            t0 = 0
            while t0 < ntok:
                out.append((t0, min(512, ntok - t0)))
                t0 += 512
            return out

        def ht_tiles(t0, n):
            return [tl for b in range(t0 // 512, (t0 + n - 1) // 512 + 1) for tl in tHT[b]]

        last_cv = [None]

        def ffn(l, cond, blocks, out_tiles, ntiles_norm):
            P.snapshot()
            norm_mod(l, 1, ntiles_norm, cond)
            P.barrier(snapshot=False)
            UT = carve(0, 32768).rearrange("p (b i t) -> p b i t", b=2, i=8)
            tUT = [Tile("UT0"), Tile("UT1")]
            WD = carve(32768, 32768).rearrange("p (b i n) -> p b i n", b=2, i=8)
            tWD = [Tile("WD0"), Tile("WD1")]
            CV = carve(65536, 6 * 4096).bitcast(F32).rearrange("p (s t) -> p s t", t=1024)
            tCV = [Tile(f"CV{i}") for i in range(6)]
            last_cv[0] = (CV, tCV)
            gi = 2 * l + 1
            pieces = [(0, 8), (8, 8), (16, 6)]
            wc0 = R_WC + (l * 3) * 44
            bc0 = R_BC + l * 44
            panels = [(c0 + pn, min(2, npc - pn)) for (c0, npc) in pieces for pn in range(0, npc, 2)]
            cidx = [0]

            def issue_panel(idx):
                if idx >= len(panels):
                    return
                cj, nch = panels[idx]
                wbuf = idx % 3
                for (co, src0) in ((0, 128 * cj), (256, FF + 128 * cj)):
                    srcw = d_wup[l][:, src0:src0 + 128 * nch].rearrange("(k p) n -> p k n", p=128)
                    P.dma("pool", lambda e, co=co, srcw=srcw: e.dma_start(
                        out=WP[:, wbuf, :, co:co + 128 * nch], in_=srcw), writes=[tWP[wbuf]])

            def issue_wd(pi):
                if pi >= len(pieces):
                    return
                c0_, npc_ = pieces[pi]
                src = d_wdn[l][128 * c0_:128 * (c0_ + npc_), :].rearrange("(i p) n -> p i n", p=128)
                P.dma("pool", lambda e: e.dma_start(out=WD[:, pi % 2, 0:npc_, :], in_=src),
                      writes=[tWD[pi % 2]], guard=True)

            def up_set(pi, j, jj, oc, wbuf, blk):
                (a0, a1, o0, o1, L) = blk
                ub = pi % 2
                na = a1 - a0
                bg, bv = nb(), nb()
                s_ = (cidx[0] % 2) * 3
                cidx[0] += 1
                for (bk, vo) in ((bg, 0), (bv, 256)):
                    P.op("pe", [lambda e, k=k, bk=bk, vo=vo: e.matmul(
                        banks[bk][:, 0:na], lhsT=WP[:, wbuf, k, vo + 128 * oc:vo + 128 * oc + 128],
                        rhs=HT[:, k, a0:a1], start=(k == 0), stop=(k == 7)) for k in range(8)],
                        reads=[tWP[wbuf]] + ht_tiles(a0, na), writes=[tB[bk]])
                for (bk, so, fo) in ((bg, 0, 0), (bv, 1, 22)):
                    w0 = CT[:, wc0 + fo + j:wc0 + fo + j + 1]
                    w1 = CT[:, wc0 + 44 + fo + j:wc0 + 44 + fo + j + 1]
                    w2 = CT[:, wc0 + 88 + fo + j:wc0 + 88 + fo + j + 1]
                    bb = CT[:, bc0 + fo + j:bc0 + fo + j + 1]
                    cv = s_ + so
                    P.op("act", lambda e, bk=bk, cv=cv, w1=w1, bb=bb: e.activation(
                        out=CV[:, cv, o0:o1], in_=banks[bk][:, o0 - a0:o1 - a0], func=AF.Identity,
                        scale=w1, bias=bb), reads=[tB[bk], tCT], writes=[tCV[cv]])
                    if L and (o1 - o0) % L == 0 and a0 == o0 and a1 == o1:
                        ov = CV[:, cv, o0:o1].rearrange("p (g t) -> p g t", t=L)
                        pv = banks[bk][:, 0:na].rearrange("p (g t) -> p g t", t=L)
                        P.op("dve", lambda e, ov=ov, pv=pv, w0=w0: e.scalar_tensor_tensor(
                            out=ov[:, :, 1:L], in0=pv[:, :, 0:L - 1], scalar=w0, in1=ov[:, :, 1:L],
                            op0=ALU.mult, op1=ALU.add), reads=[tB[bk], tCV[cv], tCT], writes=[tCV[cv]])
                        P.op("dve", lambda e, ov=ov, pv=pv, w2=w2: e.scalar_tensor_tensor(
                            out=ov[:, :, 0:L - 1], in0=pv[:, :, 1:L], scalar=w2, in1=ov[:, :, 0:L - 1],
                            op0=ALU.mult, op1=ALU.add), reads=[tB[bk], tCV[cv], tCT], writes=[tCV[cv]])
                    else:
                        l0 = max(o0, 1, a0 + 1)
                        P.op("dve", lambda e, bk=bk, cv=cv, w0=w0, l0=l0: e.scalar_tensor_tensor(
                            out=CV[:, cv, l0:o1], in0=banks[bk][:, l0 - 1 - a0:o1 - 1 - a0], scalar=w0,
                            in1=CV[:, cv, l0:o1], op0=ALU.mult, op1=ALU.add),
                            reads=[tB[bk], tCV[cv], tCT], writes=[tCV[cv]])
                        r1 = min(o1, a1 - 1)
                        P.op("dve", lambda e, bk=bk, cv=cv, w2=w2, r1=r1: e.scalar_tensor_tensor(
                            out=CV[:, cv, o0:r1], in0=banks[bk][:, o0 + 1 - a0:r1 + 1 - a0], scalar=w2,
                            in1=CV[:, cv, o0:r1], op0=ALU.mult, op1=ALU.add),
                            reads=[tB[bk], tCV[cv], tCT], writes=[tCV[cv]])
                P.op("act", lambda e: e.activation(out=CV[:, s_ + 2, o0:o1], in_=CV[:, s_, o0:o1], func=AF.Silu),
                     reads=[tCV[s_]], writes=[tCV[s_ + 2]])
                P.op("dve", lambda e: e.tensor_tensor(
                    out=UT[:, ub, jj, o0:o1], in0=CV[:, s_ + 2, o0:o1], in1=CV[:, s_ + 1, o0:o1], op=ALU.mult),
                    reads=[tCV[s_ + 2], tCV[s_ + 1]], writes=[tUT[ub]])

            def down_unit(pi, t, hh):
                npc_ = pieces[pi][1]
                bk = nb()
                P.op("pe", [lambda e, i=i: e.matmul(
                    banks[bk][:], lhsT=UT[:, pi % 2, i, 128 * t:128 * t + 128],
                    rhs=WD[:, pi % 2, i, 512 * hh:512 * hh + 512], start=(i == 0), stop=(i == npc_ - 1))
                    for i in range(npc_)], reads=[tUT[pi % 2], tWD[pi % 2]], writes=[tB[bk]])
                residual(bk, t, hh, gi)

            issue_panel(0)
            issue_panel(1)
            issue_wd(0)
            pidx = 0
            for pi, (c0, npc) in enumerate(pieces):
                sets = []
                for pn in range(0, npc, 2):
                    for oc in range(min(2, npc - pn)):
                        for blk in blocks:
                            sets.append((pn, oc, blk))
                downs = [(t, hh) for t in out_tiles for hh in range(2)] if pi > 0 else []
                dptr = 0
                last_pn = None
                for si, (pn, oc, blk) in enumerate(sets):
                    if pn != last_pn:
                        wbuf = pidx % 3
                        issue_panel(pidx + 2)
                        pidx += 1
                        last_pn = pn
                    run_deferred(1)
                    up_set(pi, c0 + pn + oc, pn + oc, oc, wbuf, blk)
                    want = ((si + 1) * len(downs)) // len(sets)
                    while dptr < want:
                        down_unit(pi - 1, *downs[dptr])
                        dptr += 1
                issue_wd(pi + 1)
            for t in out_tiles:
                for hh in range(2):
                    down_unit(len(pieces) - 1, t, hh)
            wp_rr[0] = pidx % 3

        def gmlp(cond, ntiles):
            ntok = 128 * ntiles
            P.snapshot()
            norm_mod(1, 0, ntiles, cond)
            P.barrier(snapshot=False)
            load_bsb()
            UT = carve(0, 32768).rearrange("p (g t) -> p g t", t=1024)
            tUT = [Tile(f"GU{b}") for b in range(8)]
            VV = carve(32768, 32768).rearrange("p (t f) -> p t f", f=2048)
            tVV = [Tile(f"VV{t}") for t in range(8)]
            WST = carve(65536, 4096).rearrange("p (g q) -> p g q", q=128)
            tWST = Tile("WST")
            GVB = carve(69632, 8192).bitcast(F32)
            tGVB = Tile("GVB")
            BSG = carve(77824, 512).bitcast(F32)
            tBSG = Tile("BSG")
            ST6 = carve(78336, 8 * 4 * 6 * 4).bitcast(F32).rearrange("p (t n s) -> p t n s", n=4, s=6)
            MV = carve(79104, 8 * 2 * 4).bitcast(F32).rearrange("p (t s) -> p t s", s=2)
            LR = carve(79168, 8 * 4).bitcast(F32)
            LB = carve(79200, 8 * 4).bitcast(F32)
            tST = Tile("ST")
            tLN = Tile("LN")
            P.dma("sp", lambda e: e.dma_start(out=GVB, in_=bass.AP(t_gv, 0, [[0, 128], [1, EE]])), writes=[tGVB])
            wi = next_wp()
            WSv = WP[:, wi, :, :].rearrange("p k n -> p (k n)").rearrange("p (g q) -> p g q", q=128)
            P.dma("pool", lambda e: e.dma_start(out=WSv[:, 0:16, :], in_=d_wsp.rearrange("g p q -> p g q")),
                  writes=[tWP[wi]])
            for half in range(2):
                bk = nb()
                P.op("pe", [lambda e, g=g, bk=bk, half=half: e.transpose(
                    out=banks[bk][:].bitcast(BF16)[:, 128 * g:128 * g + 128], in_=WSv[:, 8 * half + g, :],
                    identity=IDB[:]) for g in range(8)], reads=[tWP[wi], tID], writes=[tB[bk]])
                P.op("dve", lambda e, bk=bk, half=half: e.tensor_copy(
                    out=WST[:, 8 * half:8 * half + 8, :],
                    in_=banks[bk][:].bitcast(BF16).rearrange("p (g q) -> p g q", q=128)),
                    reads=[tB[bk]], writes=[tWST])
            for n in range(4):
                wv = load_panel(d_wgi, EE + 512 * n, 512)
                for t in range(ntiles):
                    bk = nb()
                    P.op("pe", [lambda e, k=k, t=t, bk=bk, wv=wv: e.matmul(
                        banks[bk][:], lhsT=HT[:, k, 128 * t:128 * t + 128], rhs=WP[:, wv, k, :],
                        start=(k == 0), stop=(k == 7)) for k in range(8)],
                        reads=[tWP[wv]] + ht_tiles(128 * t, 128), writes=[tB[bk]])
                    P.op("act", lambda e, t=t, n=n, bk=bk: e.activation(
                        out=VV[:, t, 512 * n:512 * n + 512], in_=banks[bk][:], func=AF.Gelu_apprx_tanh),
                        reads=[tB[bk]], writes=[tVV[t]])
                    P.op("dve", lambda e, t=t, n=n: e.bn_stats(out=ST6[:, t, n, :], in_=VV[:, t, 512 * n:512 * n + 512]),
                         reads=[tVV[t]], writes=[tST])
            for t in range(ntiles):
                P.op("dve", lambda e, t=t: e.bn_aggr(out=MV[:, t, :], in_=ST6[:, t, :, :].rearrange("p n s -> p (n s)")),
                     reads=[tST], writes=[tLN])
            rstd_from_ss(ntiles, 1.0, tLN, MV[:, 0:ntiles, 1], LR[:, 0:ntiles], tLN)
            P.op("dve", lambda e: e.scalar_tensor_tensor(out=LB[:, 0:ntiles], in0=MV[:, 0:ntiles, 0], scalar=-1.0,
                                                         in1=LR[:, 0:ntiles], op0=ALU.mult, op1=ALU.mult),
                 reads=[tLN], writes=[tLN])
            for t in range(ntiles):
                P.op("dve", lambda e, t=t: e.tensor_scalar(out=VV[:, t, :], in0=VV[:, t, :], scalar1=LR[:, t:t + 1],
                                                           scalar2=LB[:, t:t + 1], op0=ALU.mult, op1=ALU.add),
                     reads=[tVV[t], tLN], writes=[tVV[t]])
                P.op("dve", lambda e, t=t: e.tensor_tensor(out=VV[:, t, :], in0=VV[:, t, :], in1=GVB, op=ALU.mult),
                     reads=[tVV[t], tGVB], writes=[tVV[t]])
            for n in range(4):
                wu = load_panel(d_wgi, 512 * n, 512)
                for oc in range(4):
                    g = 4 * n + oc
                    for (t0, nt) in tok_blocks(ntok):
                        bk = nb()
                        P.op("pe", [lambda e, k=k, bk=bk, wu=wu, oc=oc, t0=t0, nt=nt: e.matmul(
                            banks[bk][:, 0:nt], lhsT=WP[:, wu, k, 128 * oc:128 * oc + 128], rhs=HT[:, k, t0:t0 + nt],
                            start=(k == 0), stop=(k == 7)) for k in range(8)],
                            reads=[tWP[wu]] + ht_tiles(t0, nt), writes=[tB[bk]])
                        P.op("act", lambda e, g=g, bk=bk, t0=t0, nt=nt: e.activation(
                            out=UT[:, g, t0:t0 + nt], in_=banks[bk][:, 0:nt], func=AF.Gelu_apprx_tanh),
                            reads=[tB[bk]], writes=[tUT[i] for i in range(t0 // 128, (t0 + nt) // 128)])
            for t in range(ntiles):
                for gq in range(4):
                    bk = nb()
                    fns = []
                    for gg in range(4):
                        g = 4 * gq + gg
                        fns.append(lambda e, g=g, gg=gg, t=t, bk=bk: e.matmul(
                            banks[bk][:, 128 * gg:128 * gg + 128], lhsT=VV[:, t, 128 * g:128 * g + 128],
                            rhs=WST[:, g, :], start=True, stop=True))
                    P.op("pe", fns, reads=[tVV[t], tWST], writes=[tB[bk]])
                    P.op("dve", lambda e, gq=gq, t=t, bk=bk: e.tensor_tensor(
                        out=TMP[:, (4 * t + gq) % 2, :].rearrange("p (g q) -> p g q", q=128),
                        in0=banks[bk][:].rearrange("p (g q) -> p g q", q=128),
                        in1=BSB[:, 4 * gq:4 * gq + 4, :], op=ALU.add),
                        reads=[tB[bk], tBSB], writes=[tTMP[(4 * t + gq) % 2]])
                    P.op("dve", lambda e, gq=gq, t=t: e.tensor_tensor(
                        out=UT[:, 4 * gq:4 * gq + 4, 128 * t:128 * t + 128],
                        in0=UT[:, 4 * gq:4 * gq + 4, 128 * t:128 * t + 128],
                        in1=TMP[:, (4 * t + gq) % 2, :].rearrange("p (g q) -> p g q", q=128), op=ALU.mult),
                        reads=[tTMP[(4 * t + gq) % 2], tUT[t]], writes=[tUT[t]])
            WOv = [WP[:, i, :, :].rearrange("p k n -> p (k n)").rearrange("p (g m) -> p g m", m=1024)
                   for i in range(4)]
            for i in range(4):
                src = d_wgo[512 * i:512 * i + 512, :].rearrange("(g p) n -> p g n", p=128)
                P.dma("pool", lambda e, i=i, src=src: e.dma_start(out=WOv[i], in_=src),
                      writes=[tWP[i]] + (tDB if i == 3 else []))
            for t in range(ntiles):
                for hh in range(2):
                    bk = nb()
                    P.op("pe", [lambda e, g=g, t=t, hh=hh, bk=bk: e.matmul(
                        banks[bk][:], lhsT=UT[:, g, 128 * t:128 * t + 128],
                        rhs=WOv[g // 4][:, g % 4, 512 * hh:512 * hh + 512], start=(g == 0), stop=(g == 15))
                        for g in range(16)], reads=[tUT[t]] + tWP, writes=[tB[bk]])
                    residual(bk, t, hh, 2)

        BSB = carve(81920, 8192).bitcast(F32).rearrange("p (g q) -> p g q", q=128)
        tBSB = Tile("BSB")

        def load_bsb():
            P.dma("sp", lambda e: e.dma_start(
                out=BSB, in_=bass.AP(d_bsp.tensor, 0, [[0, 128], [128, 16], [1, 128]])), writes=[tBSB])

        def final_norm(ntiles, dst):
            GF = carve(90112, 4096).bitcast(F32)
            tGF = Tile("GF")
            CV, tCV = last_cv[0]
            P.dma("sp", lambda e: e.dma_start(out=GF, in_=bass.AP(t_gfin, 0, [[0, 128], [1, D]])), writes=[tGF])
            for tb in range(0, ntiles, 4):
                te = min(ntiles, tb + 4)
                for t in range(tb, te):
                    P.op("act", lambda e, t=t: e.activation(out=XS[:, 0, :], in_=X[:, t, :], func=AF.Square,
                                                            accum_out=SS[:, t:t + 1]),
                         reads=[tX[t]], writes=[tXS[0], tJ, tSS])
                rstd_from_ss(te - tb, 1.0 / D, tSS, SS[:, tb:te], RS[:, tb:te], tRS)
                for t in range(tb, te):
                    ci = t % 6
                    P.op("dve", lambda e, t=t, ci=ci: e.scalar_tensor_tensor(
                        out=CV[:, ci, :], in0=X[:, t, :], scalar=RS[:, t:t + 1], in1=GF, op0=ALU.mult, op1=ALU.mult),
                        reads=[tX[t], tRS, tGF], writes=[tCV[ci]])
                    P.dma("sp", lambda e, t=t, ci=ci: e.dma_start(out=dst[128 * t:128 * t + 128, :], in_=CV[:, ci, :]),
                          reads=[tCV[ci]], writes=[Tile("out")])

        def qkv(ntq, ntk, QT, KT, VX, tQT, tKT, tVX, prompt):
            P.op("pool", lambda e: e.memset(VX[:, :, :, 64:65], 1.0), writes=tVX, guard=True)
            chk("q_memset")
            if prompt:
                KF = carve(72000, 4096).bitcast(F32).rearrange("p (s t) -> p s t", t=512)
                KS2 = carve(76096, 8192).bitcast(F32).rearrange("p (s t) -> p s t", t=512)
                VS = carve(84288, 10240).bitcast(F32).rearrange("p (s t) -> p s t", t=512)
                tKF = [Tile("KF0"), Tile("KF1")]
                tKS = [Tile(f"KS{i}") for i in range(4)]
                tVS = [Tile(f"VS{i}") for i in range(5)]
            cnt = 0
            k_pend = [None]
            for which, ntok, DST, tD in ((0, 128 * ntq, QT, tQT), (1, 128 * ntk, KT, tKT)):
                for n in range(2):
                    wi = load_panel(d_wqkv, 1024 * which + 512 * n, 512)
                    for oc in range(4):
                        c = 4 * n + oc
                        run_deferred(1)
                        for (t0, nt) in tok_blocks(ntok):
                            bk = nb()
                            P.op("pe", [lambda e, k=k, bk=bk, wi=wi, oc=oc, t0=t0, nt=nt: e.matmul(
                                banks[bk][:, 0:nt], lhsT=WP[:, wi, k, 128 * oc:128 * oc + 128],
                                rhs=HT[:, k, t0:t0 + nt], start=(k == 0), stop=(k == 7)) for k in range(8)],
                                reads=[tWP[wi]] + ht_tiles(t0, nt), writes=[tB[bk]])
                            sc = 0.125 if which == 0 else 1.0
                            P.op("act", lambda e, c=c, bk=bk, t0=t0, nt=nt, DST=DST, sc=sc: e.activation(
                                out=DST[:, c, t0:t0 + nt], in_=banks[bk][:, 0:nt], func=AF.Copy, scale=sc),
                                reads=[tB[bk]], writes=[tD[c]])
                            if prompt and which == 1:
                                s = cnt % 2
                                ks = cnt % 4
                                cnt += 1
                                P.op("dve", lambda e, s=s, bk=bk: e.tensor_copy(out=KF[:, s, :], in_=banks[bk][:]),
                                     reads=[tB[bk]], writes=[tKF[s]])

                                def later(s=s, c=c, t0=t0, ks=ks):
                                    b2 = nb()
                                    P.op("pe", [lambda e, i=i: e.transpose(
                                        out=banks[b2][:, 128 * i:128 * i + 128], in_=KF[:, s, 128 * i:128 * i + 128],
                                        identity=IDF[:]) for i in range(4)], reads=[tKF[s], tID], writes=[tB[b2]])
                                    P.op("dve", lambda e: e.tensor_copy(out=KS2[:, ks, :], in_=banks[b2][:]),
                                         reads=[tB[b2]], writes=[tKS[ks]])
                                    for i in range(4):
                                        t = t0 // 128 + i
                                        dst = d_sk[t // 2, 2 * c:2 * c + 2, 128 * (t % 2):128 * (t % 2) + 128, :].rearrange(
                                            "h s d -> s h d")
                                        P.dma("sp", lambda e, i=i, dst=dst: e.dma_start(
                                            out=dst, in_=KS2[:, ks, 128 * i:128 * i + 128].rearrange("p (h d) -> p h d", d=64)),
                                            reads=[tKS[ks]], writes=[Tile("out")])

                                prev_later = k_pend[0]
                                k_pend[0] = later
                                if prev_later is not None:
                                    prev_later()
                if k_pend[0] is not None:
                    k_pend[0]()
                    k_pend[0] = None
                chk("q_q" if which == 0 else "q_k")
            for n in range(2):
                wi = load_panel(d_wqkv, 2048 + 512 * n, 512)
                for t in range(ntk):
                    run_deferred(1)
                    bk = nb()
                    P.op("pe", [lambda e, k=k, t=t, bk=bk, wi=wi: e.matmul(
                        banks[bk][:], lhsT=HT[:, k, 128 * t:128 * t + 128], rhs=WP[:, wi, k, :],
                        start=(k == 0), stop=(k == 7)) for k in range(8)],
                        reads=[tWP[wi]] + ht_tiles(128 * t, 128), writes=[tB[bk]])
                    P.op("act", lambda e, t=t, n=n, bk=bk: e.activation(
                        out=VX[:, t, 8 * n:8 * n + 8, 0:64], in_=banks[bk][:].rearrange("p (h d) -> p h d", d=64),
                        func=AF.Copy), reads=[tB[bk]], writes=[tVX[t]])
                    if prompt:
                        s = cnt % 5
                        cnt += 1
                        P.op("dve", lambda e, s=s, bk=bk: e.tensor_copy(out=VS[:, s, :], in_=banks[bk][:]),
                             reads=[tB[bk]], writes=[tVS[s]])
                        dst = d_sv[t // 2, 8 * n:8 * n + 8, 128 * (t % 2):128 * (t % 2) + 128, :].rearrange("h s d -> s h d")
                        P.dma("sp", lambda e, s=s, dst=dst: e.dma_start(
                            out=dst, in_=VS[:, s, :].rearrange("p (h d) -> p h d", d=64)),
                            reads=[tVS[s]], writes=[Tile("out")])

        def o_transpose(O, tO, oi, tile):
            bk = nb()
            P.op("pe", [lambda e, c=c: e.transpose(out=banks[bk][:].bitcast(BF16)[:, 128 * c:128 * c + 128],
                                                   in_=O[:, oi, 128 * c:128 * c + 128], identity=IDB[:])
                        for c in range(8)], reads=[tO[oi], tID], writes=[tB[bk]])
            P.op("act", lambda e: e.activation(
                out=HT[:, :, 128 * tile:128 * tile + 128],
                in_=banks[bk][:].bitcast(BF16).rearrange("p (c t) -> p c t", t=128), func=AF.Copy),
                reads=[tB[bk]], writes=list(tHT[tile // 4]))

        def out_proj(ntiles):
            flush_deferred()
            wis = [load_panel(d_wo, 512 * n, 512) for n in range(2)]
            for t in range(ntiles):
                for n in range(2):
                    wi = wis[n]
                    bk = nb()
                    P.op("pe", [lambda e, k=k, t=t, bk=bk, wi=wi: e.matmul(
                        banks[bk][:], lhsT=HT[:, k, 128 * t:128 * t + 128], rhs=WP[:, wi, k, :],
                        start=(k == 0), stop=(k == 7)) for k in range(8)],
                        reads=[tWP[wi]] + ht_tiles(128 * t, 128), writes=[tB[bk]])
                    residual(bk, t, n, 0)

        def pv_finish(bk, O, oi, tO, h, RC, tRC):
            P.op("dve", lambda e: e.reciprocal(out=RC[:, 0:1], in_=banks[bk][:, 64:65]), reads=[tB[bk]], writes=[tRC])
            P.op("dve", lambda e: e.tensor_scalar(out=O[:, oi, 64 * h:64 * h + 64], in0=banks[bk][:, 0:64],
                                                  scalar1=RC[:, 0:1], scalar2=None, op0=ALU.mult),
                 reads=[tB[bk], tRC], writes=[tO[oi]])

        def prompt_group():
            cond = 0
            for t in range(8):
                P.dma("sp", lambda e, t=t: e.dma_start(out=X[:, t, :], in_=d_xp[128 * t:128 * t + 128, :]),
                      writes=[tX[t]])
            P.snapshot()
            norm_mod(0, 0, 8, cond)
            P.barrier(snapshot=False)
            chk("p_norm")
            QT = carve(0, 16384).rearrange("p (c t) -> p c t", t=1024)
            KT = carve(16384, 16384).rearrange("p (c t) -> p c t", t=1024)
            VX = carve(32768, 8 * 16 * 65 * 2).rearrange("p (t h d) -> p t h d", h=16, d=65)
            O = carve(49408, 4096).rearrange("p (s f) -> p s f", f=1024)
            PT = carve(53504, 6144).rearrange("p (s k) -> p s k", k=1024)
            RC = carve(59648, 64).bitcast(F32)
            tQT, tKT = [Tile(f"QT{c}") for c in range(8)], [Tile(f"KT{c}") for c in range(8)]
            tVX = [Tile(f"VX{t}") for t in range(8)]
            tO = [Tile("O0"), Tile("O1")]
            tPT = [Tile("PT0"), Tile("PT1"), Tile("PT2")]
            tRC = Tile("RC")
            qkv(8, 8, QT, KT, VX, tQT, tKT, tVX, True)
            chk("p_qkv")
            def p_stage1(u, pi):
                b, qc, hg = u
                tile = 2 * b + qc
                bks = [nb(), nb()]
                fns = []
                for hh in range(4):
                    h = 4 * hg + hh
                    c, po = h // 2, 64 * (h % 2)
                    for kc in range(2):
                        fns.append(lambda e, hh=hh, kc=kc, c=c, po=po: e.matmul(
                            banks[bks[hh % 2]][:, 256 * (hh // 2) + 128 * kc:256 * (hh // 2) + 128 * kc + 128],
                            lhsT=KT[po:po + 64, c, 128 * (2 * b + kc):128 * (2 * b + kc) + 128],
                            rhs=QT[po:po + 64, c, 128 * tile:128 * tile + 128], start=True, stop=True))
                P.op("pe", fns, reads=[tQT[2 * hg], tKT[2 * hg], tQT[2 * hg + 1], tKT[2 * hg + 1]],
                     writes=[tB[bks[0]], tB[bks[1]]])
                for i2 in range(2):
                    P.op("act", lambda e, i2=i2: e.activation(out=PT[:, pi, 512 * i2:512 * i2 + 512],
                                                              in_=banks[bks[i2]][:], func=AF.Exp),
                         reads=[tB[bks[i2]]], writes=[tPT[pi]])

            def p_stage2(u, pi):
                b, qc, hg = u
                tile = 2 * b + qc
                oi = tile % 2
                b2 = nb()
                fns = []
                for hh in range(4):
                    h = 4 * hg + hh
                    for kc in range(2):
                        fns.append(lambda e, hh=hh, kc=kc, h=h: e.matmul(
                            banks[b2][:, 65 * hh:65 * hh + 65],
                            lhsT=PT[:, pi, 512 * (hh % 2) + 256 * (hh // 2) + 128 * kc:
                                    512 * (hh % 2) + 256 * (hh // 2) + 128 * kc + 128], rhs=VX[:, 2 * b + kc, h, :],
                            start=(kc == 0), stop=(kc == 1)))
                P.op("pe", fns, reads=[tPT[pi], tVX[2 * b], tVX[2 * b + 1]], writes=[tB[b2]])
                pvv = banks[b2][:, 0:260].rearrange("p (h d) -> p h d", d=65)
                P.op("dve", lambda e: e.reciprocal(out=RC[:, 0:4], in_=pvv[:, :, 64]), reads=[tB[b2]], writes=[tRC])
                P.op("dve", lambda e: e.tensor_tensor(
                    out=O[:, oi, 256 * hg:256 * hg + 256].rearrange("p (h d) -> p h d", d=64), in0=pvv[:, :, 0:64],
                    in1=RC[:, 0:4].unsqueeze(2).to_broadcast([128, 4, 64]), op=ALU.mult),
                    reads=[tB[b2], tRC], writes=[tO[oi]])
                if hg == 3:
                    o_transpose(O, tO, oi, tile)

            units = [(b, qc, hg) for b in range(4) for qc in range(2) for hg in range(4)]

            g_first, g_items = gate_riders(cond)
            g_first()
            riders = g_items + [ada_rider(0, n) for n in (6, 7, 8, 9)] + \
                [ada_rider(1, n) for n in (0, 1, 2, 3, 6, 7, 8, 9)]
            r_step, r_state = start_riders(riders, nbuf=4)
            for i, u in enumerate(units):
                if i >= 2 and i % 8 not in (3, 7):
                    r_step()
                p_stage1(u, i % 3)
                if i >= 2:
                    p_stage2(units[i - 2], (i - 2) % 3)
            assert r_state["next"] == len(riders)
            am_item(0, 1)[1](None)
            am_item(1, 0)[1](None)
            am_item(1, 1)[1](None)
            wp_rr[0] = 0
            for i in (len(units) - 2, len(units) - 1):
                p_stage2(units[i], i % 3)
            chk("p_attn")
            out_proj(8)
            chk("p_oproj")
            blocks = [(0, 512, 0, 512, 256), (512, 1024, 512, 1024, 256)]
            ffn(0, cond, blocks, list(range(8)), 8)
            flush_deferred()
            chk("p_ffn0")
            gmlp(cond, 8)
            chk("p_gmlp")
            ffn(1, cond, blocks, list(range(8)), 8)
            chk("p_ffn1")
            final_norm(8, d_yp)
            chk("p_final")

        def sample_group():
            cond = 1
            for t in range(8):
                P.dma("sp", lambda e, t=t: e.dma_start(out=X[:, t, :], in_=d_xs[128 * t:128 * t + 128, :]),
                      writes=[tX[t]])
            P.snapshot()
            norm_mod(0, 0, 8, cond)
            P.barrier(snapshot=False)
            QT = carve(0, 12288).rearrange("p (c t) -> p c t", t=768)
            KT = carve(12288, 16384).rearrange("p (c t) -> p c t", t=1024)
            VX = carve(28672, 8 * 16 * 65 * 2).rearrange("p (t h d) -> p t h d", h=16, d=65)
            OA = carve(45312, 12288).rearrange("p (s f) -> p s f", f=1024)
            KCT = carve(57600, 8192).rearrange("p (c t) -> p c t", t=512)
            VCX = carve(65792, 4 * 16 * 65 * 2).rearrange("p (a h d) -> p a h d", h=16, d=65)
            MINT = carve(74112, 2560).rearrange("p (s e k) -> p s e k", s=2, k=64)
            MTOP = carve(76672, 2560).rearrange("p (s e k) -> p s e k", s=2, k=64)
            PT = carve(79232, 2 * 1152 * 2).rearrange("p (s k) -> p s k", k=1152)
            GR = carve(83840, 2 * 3072).bitcast(F32).rearrange("p (s e k) -> p s e k", s=2, k=64)
            G2 = carve(89984, 2 * 2 * 1280).rearrange("p (w s e k) -> p w s e k", s=2, w=2, k=64)
            RC = carve(95104, 16).bitcast(F32)
            RPT = TMP[:, 0, 0:320].rearrange("p (a x) -> p a x", x=160)
            tQT, tKT = [Tile(f"QTs{c}") for c in range(8)], [Tile(f"KTs{c}") for c in range(8)]
            tVX = [Tile(f"VXs{t}") for t in range(8)]
            tOA = [Tile(f"OA{m}") for m in range(6)]
            tKCT, tVCX, tMSK = Tile("KCT"), Tile("VCX"), Tile("MSK")
            tPT = [Tile("PTs0"), Tile("PTs1")]
            tGR = [Tile("GR0"), Tile("GR1")]
            tG2 = [Tile("G20"), Tile("G21")]
            tRC, tRPT, tRP = Tile("RCs"), tTMP[0], Tile("RP")
            chk("s_pre")
            qkv(6, 8, QT, KT, VX, tQT, tKT, tVX, False)
            P.op("pool", lambda e: e.memset(RPT[0:120, :, :], 0.0), writes=[tRPT])
            P.dma("sp", lambda e: e.dma_start(out=RPT[0:120, :, 48:79], in_=d_rpb.rearrange("(a p) j -> p a j", p=120)),
                  reads=[tRPT], writes=[tRPT])
            P.dma("sp", lambda e: e.dma_start(out=t_rp.ap().rearrange("(a p) x -> p a x", p=120), in_=RPT[0:120, :, :]),
                  reads=[tRPT], writes=[tRP])
            P.dma("pool", lambda e: e.dma_start(out=MINT, in_=d_mint.rearrange("p (s e k) -> p s e k", s=2, k=64)),
                  writes=[tMSK], guard=True)
            P.dma("pool", lambda e: e.dma_start(out=MTOP, in_=d_mtop.rearrange("p (s e k) -> p s e k", s=2, k=64)),
                  writes=[tMSK], guard=True)
            wk = 3
            KC = WP[:, wk, :, :].rearrange("p k n -> p (k n)").rearrange("p (a h d) -> p a h d", h=16, d=64)
            for a in range(4):
                P.dma("pool", lambda e, a=a: e.dma_start(
                    out=KC[:, a, :, :], in_=d_ck[:, 128 * a:128 * a + 128, :].rearrange("h p d -> p h d")),
                    writes=[tWP[wk]] + tDB)
                P.dma("pool", lambda e, a=a: e.dma_start(
                    out=VCX[:, a, :, 0:64], in_=d_cv[:, 128 * a:128 * a + 128, :].rearrange("h p d -> p h d")),
                    writes=[tVCX], guard=True)
            P.op("pool", lambda e: e.memset(VCX[:, :, :, 64:65], 1.0), writes=[tVCX], guard=True)
            for a in range(4):
                bk = nb()
                P.op("pe", [lambda e, hp=hp, a=a, bk=bk: e.transpose(
                    out=banks[bk][:].bitcast(BF16)[:, 128 * hp:128 * hp + 128],
                    in_=KC[:, a, 2 * hp:2 * hp + 2, :].rearrange("p h d -> p (h d)"), identity=IDB[:])
                    for hp in range(8)], reads=[tWP[wk], tID] + tDB, writes=[tB[bk]])
                P.op("dve", lambda e, a=a, bk=bk: e.tensor_copy(
                    out=KCT[:, :, 128 * a:128 * a + 128],
                    in_=banks[bk][:].bitcast(BF16).rearrange("p (c t) -> p c t", t=128)),
                    reads=[tB[bk]], writes=[tKCT])
            chk("s_qkv")
            def s_plan(m):
                if m < 2:
                    return [0, 1, 2, 3], 1, [2 * j - 2 * m + 2 for j in range(4)]
                return list(range(m - 2, m + 3)), 0, [2 * (j - m) + 4 for j in range(m - 2, m + 3)]

            def build_tables(h):
                po = 64 * (h % 2)
                gs = h % 2
                for s_ in range(2):
                    src = bass.AP(t_rp, (h * 15 + 3 - s_) * 160, [[1, 64], [160, 12], [1, 64]])
                    P.dma("sp", lambda e, s_=s_, src=src: e.dma_start(
                        out=GR[po:po + 64, s_, :, :], in_=src), reads=[tRP], writes=[tGR[gs]])
                P.op("dve", lambda e: e.tensor_tensor(out=G2[po:po + 64, 0, :, :, :], in0=GR[po:po + 64, :, 0:10, :],
                                                      in1=MINT[po:po + 64, :, :, :], op=ALU.add),
                     reads=[tGR[gs], tMSK], writes=[tG2[gs]])
                P.op("dve", lambda e: e.tensor_tensor(out=G2[po:po + 64, 1, :, :, :], in0=GR[po:po + 64, :, 2:12, :],
                                                      in1=MTOP[po:po + 64, :, :, :], op=ALU.add),
                     reads=[tGR[gs], tMSK], writes=[tG2[gs]])

            def s_stage1(u, pi):
                h, m = u
                c, po = h // 2, 64 * (h % 2)
                gs = h % 2
                if m == 0 and h == 0:
                    build_tables(0)
                if m == 2 and h + 1 < NH:
                    build_tables(h + 1)
                js, tw, i0 = s_plan(m)
                nw = len(js)
                nch = nw + 4
                bks = [nb() for _ in range((nch + 3) // 4)]
                fns = []
                for i in range(nch):
                    outp = banks[bks[i // 4]][:, 128 * (i % 4):128 * (i % 4) + 128]
                    if i < nw:
                        j = js[i]
                        fns.append(lambda e, outp=outp, j=j: e.matmul(
                            outp, lhsT=KT[po:po + 64, c, 128 * j:128 * j + 128],
                            rhs=QT[po:po + 64, c, 128 * m:128 * m + 128], start=True, stop=False))
                        for s_ in range(2):
                            fns.append(lambda e, outp=outp, ii=i0[i], s_=s_: e.matmul(
                                outp[:, 64 * s_:64 * s_ + 64],
                                lhsT=G2[po:po + 64, tw, s_, ii:ii + 2, :].rearrange("p e k -> p (e k)"),
                                rhs=JB[po:po + 64, po:po + 64],
                                start=False, stop=(s_ == 1)))
                    else:
                        a_ = i - nw
                        fns.append(lambda e, outp=outp, a_=a_: e.matmul(
                            outp, lhsT=KCT[po:po + 64, c, 128 * a_:128 * a_ + 128],
                            rhs=QT[po:po + 64, c, 128 * m:128 * m + 128], start=True, stop=True))
                P.op("pe", fns, reads=[tQT[c], tKT[c], tKCT, tG2[gs], tID], writes=[tB[b_] for b_ in bks])
                for bi_, bk in enumerate(bks):
                    n_here = min(4, nch - 4 * bi_) * 128
                    P.op("act", lambda e, bk=bk, bi_=bi_, n_here=n_here: e.activation(
                        out=PT[:, pi, 512 * bi_:512 * bi_ + n_here], in_=banks[bk][:, 0:n_here], func=AF.Exp),
                        reads=[tB[bk]], writes=[tPT[pi]])

            def s_stage2(u, pi):
                h, m = u
                js, tw, i0 = s_plan(m)
                nw = len(js)
                nch = nw + 4
                b2 = nb()
                fns = []
                for i in range(nch):
                    rhs = VX[:, js[i], h, :] if i < nw else VCX[:, i - nw, h, :]
                    fns.append(lambda e, i=i, rhs=rhs: e.matmul(
                        banks[b2][:, 0:65], lhsT=PT[:, pi, 128 * i:128 * i + 128], rhs=rhs,
                        start=(i == 0), stop=(i == nch - 1)))
                P.op("pe", fns, reads=[tPT[pi], tVCX] + [tVX[j] for j in js], writes=[tB[b2]])
                pv_finish(b2, OA, m, tOA, h, RC, tRC)

            units = [(h, m) for h in range(NH) for m in range(6)]
            g_first, g_items = gate_riders(cond)
            g_first()
            r_step, r_state = start_riders(g_items)
            for i, u in enumerate(units):
                if i % 8 == 4:
                    r_step()
                s_stage1(u, i % 2)
                if i:
                    s_stage2(units[i - 1], (i - 1) % 2)
            s_stage2(units[-1], (len(units) - 1) % 2)
            assert r_state["next"] == 8
            wp_rr[0] = 0
            chk("s_attn")
            for m in range(6):
                o_transpose(OA, tOA, m, m)
            out_proj(6)
            chk("s_oproj")
            ffn(0, cond, [(0, 321, 0, 320, 0), (319, 641, 320, 640, 0)], list(range(5)), 6)
            gmlp(cond, 5)
            ffn(1, cond, [(0, 257, 0, 256, 0), (255, 513, 256, 512, 0)], list(range(4)), 5)
            final_norm(4, d_ys)

        try:
            chk("setup")
            if do_prompt:
                prompt_group()
            if do_sample:
                sample_group()
        except _Stop:
            pass
        P.finish()
        with nc.Block() as block:
            P.emit(block)
    return nc


def _masks(parity):
    qc = np.arange(64)
    kc = np.arange(64)
    cs = np.clip(qc - 8, 0, 48)
    validA = (kc[None, :] >= cs[:, None]) & (kc[None, :] < cs[:, None] + 16)
    if parity:
        validA = validA[::-1, ::-1]
    colmask = np.where(validA, 0.0, NEG).astype(np.float32)
    lo, hi = (-4, 3) if parity == 0 else (-3, 4)
    mint = np.zeros((2, 64, 2, 10, 64), np.float32)
    mtop = np.zeros((2, 64, 2, 10, 64), np.float32)
    for half in range(2):
        for pp in range(64):
            q = 63 - pp
            for s in range(2):
                for ei in range(10):
                    dl = ei - 4 - s
                    mint[half, pp, s, ei] = colmask[q] if lo <= dl <= hi else NEG
                    mtop[half, pp, s, ei] = colmask[q]
    return mint.reshape(128, 1280), mtop.reshape(128, 1280)


def _core_inputs(core, I):
    par = core % 2
    b = core // 2
    rv = (lambda a, ax: np.flip(a, axis=ax)) if par else (lambda a, ax: a)
    xp = rv(I["x_prompt"][4 * core:4 * core + 4], 1).reshape(1024, D)
    xs = rv(I["x_sample"][b], 0)
    rows = np.zeros((N_ROWS, 128), np.float32)
    rows[R_BADA:R_BADA + 96] = I["b_ada"].reshape(96, 128)
    rows[R_CCTX:R_CCTX + 8] = I["c_ctx"].reshape(8, 128)
    rows[R_C:R_C + 8] = I["c"][b].reshape(8, 128)
    rows[R_GMIX:R_GMIX + 16] = I["norm_mix_g"].reshape(16, 128)
    rows[R_GFFN:R_GFFN + 16] = I["norm_ffn_g"].reshape(16, 128)
    rows[R_GFIN:R_GFIN + 8] = I["norm_final_g"].reshape(8, 128)
    wc = rv(I["w_ffn_conv"], 1)
    rows[R_WC:R_WC + 264] = wc.reshape(264, 128)
    rows[R_BC:R_BC + 88] = I["b_ffn_conv"].reshape(88, 128)
    jm = np.zeros((128, 128), np.float32)
    for s in range(2):
        for p in range(64):
            jm[64 * s + p, 64 * s + 63 - p] = 1.0
    mint, mtop = _masks(par)
    m = {
        "xp": xp, "xs": xs,
        "ck": I["cache_k"][b, 0], "cv": I["cache_v"][b, 0],
        "rows": rows, "ident": np.eye(128, dtype=np.float32), "jmat": jm,
        "mint": mint, "mtop": mtop,
        "w_ada": I["w_ada"], "b_ada": I["b_ada"],
        "w_qkv": I["w_qkv"][0], "w_o": I["w_attn_out"][0],
        "rpb": rv(rv(I["rpb"][0], 1), 2).reshape(NH * 15, 31),
        "w_gi": I["w_gmlp_in"][0], "g_v": I["g_gmlp_v"][0],
        "w_sp": rv(rv(I["w_spatial"][0], 1), 2), "b_sp": rv(I["b_spatial"][0], 1),
        "w_go": I["w_gmlp_out"][0],
        "w_up": I["w_ffn_up"], "w_dn": I["w_ffn_down"], "g_fin": I["norm_final_g"],
    }
    return {k: np.ascontiguousarray(v, dtype=np.float32) for k, v in m.items()}


_NC_CACHE = {}


def kernel(**inputs):
    I = {k: np.asarray(v) for k, v in inputs.items()}
    if "nc" not in _NC_CACHE:
        _NC_CACHE["nc"] = build_program()
    nc = _NC_CACHE["nc"]
    in_maps = [_core_inputs(c, I) for c in range(8)]
    res = run_bass_kernel_spmd(nc, in_maps, core_ids=list(range(8)))
    y_prompt = np.zeros((32, 256, D), np.float32)
    y_sample = np.zeros((4, 1024, D), np.float32)
    state_k = np.zeros((32, 1, NH, 256, 64), np.float32)
    state_v = np.zeros((32, 1, NH, 256, 64), np.float32)
    for c in range(8):
        r = res.results[c]
        par, b = c % 2, c // 2
        yp = np.asarray(r["yp"]).reshape(4, 256, D)
        ys = np.asarray(r["ys"])
        sk = np.asarray(r["sk"])
        sv = np.asarray(r["sv"])
        if par:
            yp = yp[:, ::-1]
            ys = ys[::-1]
            sk = sk[:, :, ::-1]
            sv = sv[:, :, ::-1]
            y_sample[b, 512:] = ys
        else:
            y_sample[b, :512] = ys
        y_prompt[4 * c:4 * c + 4] = yp
        state_k[4 * c:4 * c + 4, 0] = sk
        state_v[4 * c:4 * c + 4, 0] = sv
    return (y_prompt, y_sample, state_k, state_v)
```
